# Optimizing a Trainium2 kernel written in Bass

```python
import math
import jax, jax.numpy as jnp
from jax import lax
import numpy as np

D_MODEL = 1024
BATCH = 16
SEQ = 4096
DEPTH = 4

CHUNK = 64
HEAD_DIM = 64
GROUP_WIDTH = D_MODEL // 4
N_HEADS = GROUP_WIDTH // HEAD_DIM
ROPE_THETA = 10000.0
NEG_INF = -1e30

A_PREV_CHUNKS = 8
REL_CLIP = 128
MLA_NOPE_DIM = HEAD_DIM
MLA_ROPE_DIM = HEAD_DIM // 2
MLA_V_DIM = HEAD_DIM
MLA_Q_RANK = 192
MLA_KV_RANK = 128
Q_BLOCK = 128
SWA_WINDOW = 128
SWA_PREV_CHUNKS = SWA_WINDOW // CHUNK
SWA_KV_HEADS = 2
MEM_LEN = 256
MEM_HEADS = 4

SPLIT_WIDTHS = (
    GROUP_WIDTH, GROUP_WIDTH, GROUP_WIDTH, GROUP_WIDTH,
    MLA_Q_RANK, MLA_KV_RANK, MLA_ROPE_DIM, GROUP_WIDTH,
    GROUP_WIDTH, SWA_KV_HEADS * HEAD_DIM, SWA_KV_HEADS * HEAD_DIM, GROUP_WIDTH,
    GROUP_WIDTH, GROUP_WIDTH,
)
PROJ_WIDTH = sum(SPLIT_WIDTHS)

kernel_name = 'hybrid_chunk_causal_head_groups'


def _split_points():
    pts, acc = [], 0
    for w in SPLIT_WIDTHS[:-1]:
        acc += w
        pts.append(acc)
    return pts


def _layer_norm(x, g, b, eps=1e-5):
    xf = x.astype(jnp.float32)
    mu = jnp.mean(xf, axis=-1, keepdims=True)
    var = jnp.mean(jnp.square(xf - mu), axis=-1, keepdims=True)
    y = (xf - mu) * lax.rsqrt(var + eps) * g.astype(jnp.float32) + b.astype(jnp.float32)
    return y.astype(x.dtype)


def _rms_norm(x, g, eps=1e-6):
    xf = x.astype(jnp.float32)
    y = xf * lax.rsqrt(jnp.mean(jnp.square(xf), axis=-1, keepdims=True) + eps) * g.astype(jnp.float32)
    return y.astype(x.dtype)


def _rope(x, positions):
    d = x.shape[-1]
    inv_freq = ROPE_THETA ** (-jnp.arange(0, d, 2, dtype=jnp.float32) / d)
    ang = positions.astype(jnp.float32)[..., None] * inv_freq
    cos = jnp.cos(ang)[:, :, None, :]
    sin = jnp.sin(ang)[:, :, None, :]
    x1, x2 = jnp.split(x.astype(jnp.float32), 2, axis=-1)
    out = jnp.concatenate([x1 * cos - x2 * sin, x2 * cos + x1 * sin], axis=-1)
    return out.astype(x.dtype)


def _banded_chunk_attention(q, k, v, n_prev, rel_bias=None, sinks=None):
    b, s, h, dh = q.shape
    kvh = k.shape[2]
    g = h // kvh
    nc = s // CHUNK
    pad = n_prev * CHUNK
    band = pad + CHUNK
    kp = jnp.pad(k, ((0, 0), (pad, 0), (0, 0), (0, 0)))
    vp = jnp.pad(v, ((0, 0), (pad, 0), (0, 0), (0, 0)))
    qc = q.reshape(b, nc, CHUNK, kvh, g, dh).transpose(1, 0, 2, 3, 4, 5)
    scale = dh ** -0.5
    kk = jnp.arange(band)
    if rel_bias is not None:
        rel = jnp.arange(CHUNK)[:, None] + pad - kk[None, :]
        idx = jnp.clip(rel, -REL_CLIP, REL_CLIP) + REL_CLIP
        bias = rel_bias[:, idx].astype(jnp.float32).reshape(kvh, g, CHUNK, band)

    def one_chunk(args):
        qb, c = args
        kb = lax.dynamic_slice_in_dim(kp, c * CHUNK, band, axis=1)
        vb = lax.dynamic_slice_in_dim(vp, c * CHUNK, band, axis=1)
        sc = jnp.einsum('bqkgd,bskd->bkgqs', qb, kb).astype(jnp.float32) * scale
        if rel_bias is not None:
            sc = sc + bias
        valid = (c * CHUNK - pad + kk) >= 0
        sc = jnp.where(valid, sc, NEG_INF)
        if sinks is not None:
            sink = jnp.broadcast_to(sinks.astype(jnp.float32).reshape(kvh, g, 1, 1), sc.shape[:-1] + (1,))
            p = jax.nn.softmax(jnp.concatenate([sc, sink], axis=-1), axis=-1)[..., :band]
        else:
            p = jax.nn.softmax(sc, axis=-1)
        o = jnp.einsum('bkgqs,bskd->bqkgd', p.astype(v.dtype), vb)
        return o.reshape(b, CHUNK, h, dh)

    out = lax.map(one_chunk, (qc, jnp.arange(nc)))
    return out.transpose(1, 0, 2, 3, 4).reshape(b, s, h * dh)


def _mla(c_q, c_kv, k_rope, positions, q_norm, w_uq, kv_norm, w_ukv):
    b, s, _ = c_q.shape
    q = (_rms_norm(c_q, q_norm) @ w_uq).reshape(b, s, N_HEADS, MLA_NOPE_DIM + MLA_ROPE_DIM)
    q_nope, q_rope = q[..., :MLA_NOPE_DIM], q[..., MLA_NOPE_DIM:]
    q_rope = _rope(q_rope, positions)
    kv = (_rms_norm(c_kv, kv_norm) @ w_ukv).reshape(b, s, N_HEADS, MLA_NOPE_DIM + MLA_V_DIM)
    k_nope, v = kv[..., :MLA_NOPE_DIM], kv[..., MLA_NOPE_DIM:]
    k_r = _rope(k_rope[:, :, None, :], positions)[:, :, 0, :]
    scale = (MLA_NOPE_DIM + MLA_ROPE_DIM) ** -0.5
    nb = s // Q_BLOCK
    qn = q_nope.reshape(b, nb, Q_BLOCK, N_HEADS, MLA_NOPE_DIM).transpose(1, 0, 2, 3, 4)
    qr = q_rope.reshape(b, nb, Q_BLOCK, N_HEADS, MLA_ROPE_DIM).transpose(1, 0, 2, 3, 4)
    key_chunk = jnp.arange(s) // CHUNK

    def one_block(args):
        qn_b, qr_b, i = args
        sc = (jnp.einsum('bqhd,bshd->bhqs', qn_b, k_nope)
              + jnp.einsum('bqhd,bsd->bhqs', qr_b, k_r)).astype(jnp.float32) * scale
        q_chunk = (i * Q_BLOCK + jnp.arange(Q_BLOCK)) // CHUNK
        mask = key_chunk[None, :] <= q_chunk[:, None]
        p = jax.nn.softmax(jnp.where(mask, sc, NEG_INF), axis=-1)
        return jnp.einsum('bhqs,bshd->bqhd', p.astype(v.dtype), v)

    out = lax.map(one_block, (qn, qr, jnp.arange(nb)))
    return out.transpose(1, 0, 2, 3, 4).reshape(b, s, N_HEADS * MLA_V_DIM)


def _memory_attention(q, mem, w_mem_kv):
    b, s, _ = q.shape
    qh = q.reshape(b, s, MEM_HEADS, HEAD_DIM)
    kv = (mem @ w_mem_kv).reshape(b, MEM_LEN, 2, MEM_HEADS, HEAD_DIM)
    k, v = kv[:, :, 0], kv[:, :, 1]
    sc = jnp.einsum('bqhd,bmhd->bhqm', qh, k).astype(jnp.float32) * (HEAD_DIM ** -0.5)
    p = jax.nn.softmax(sc, axis=-1)
    return jnp.einsum('bhqm,bmhd->bqhd', p.astype(v.dtype), v).reshape(b, s, MEM_HEADS * HEAD_DIM)


def _layer(x, mem, positions, w_in, rel_bias, mla_q_norm, w_uq, mla_kv_norm, w_ukv,
           swa_sinks, w_mem_kv, w_out, ln_gain, ln_bias):
    b, s, _ = x.shape
    proj = x @ w_in
    (aq, ak, av, ag, bcq, bckv, bkr, bg, cq, ck, cv, cg, mq, mg) = jnp.split(proj, _split_points(), axis=-1)

    ya = _banded_chunk_attention(aq.reshape(b, s, N_HEADS, HEAD_DIM),
                                 ak.reshape(b, s, N_HEADS, HEAD_DIM),
                                 av.reshape(b, s, N_HEADS, HEAD_DIM),
                                 A_PREV_CHUNKS, rel_bias=rel_bias)
    yb = _mla(bcq, bckv, bkr, positions, mla_q_norm, w_uq, mla_kv_norm, w_ukv)
    qc = _rope(cq.reshape(b, s, N_HEADS, HEAD_DIM), positions)
    kc = _rope(ck.reshape(b, s, SWA_KV_HEADS, HEAD_DIM), positions)
    yc = _banded_chunk_attention(qc, kc, cv.reshape(b, s, SWA_KV_HEADS, HEAD_DIM),
                                 SWA_PREV_CHUNKS, sinks=swa_sinks)
    ym = _memory_attention(mq, mem, w_mem_kv)

    gated = jnp.concatenate([ya * jax.nn.silu(ag), yb * jax.nn.silu(bg),
                             yc * jax.nn.silu(cg), ym * jax.nn.silu(mg)], axis=-1)
    y = gated @ w_out
    alpha = (2.0 * DEPTH) ** 0.25
    return _layer_norm(alpha * x + y, ln_gain, ln_bias)


def setup_inputs(seed: int = 0) -> dict:
    key = jax.random.key(seed)
    ks = jax.random.split(key, 16)
    f32 = jnp.float32
    beta = (8.0 * DEPTH) ** -0.25
    x = jax.random.normal(ks[0], (BATCH, SEQ, D_MODEL), f32)
    mem = jax.random.normal(ks[1], (BATCH, MEM_LEN, D_MODEL), f32)
    offset = jax.random.randint(ks[2], (BATCH, 1), 0, 4096, dtype=jnp.int32)
    positions = offset + jnp.arange(SEQ, dtype=jnp.int32)[None, :]
    w_in = jax.random.normal(ks[3], (DEPTH, D_MODEL, PROJ_WIDTH), f32) * D_MODEL ** -0.5
    rel_bias = jax.random.normal(ks[4], (DEPTH, N_HEADS, 2 * REL_CLIP + 1), f32) * 0.5
    mla_q_norm = 1.0 + 0.02 * jax.random.normal(ks[5], (DEPTH, MLA_Q_RANK), f32)
    w_uq = jax.random.normal(ks[6], (DEPTH, MLA_Q_RANK, N_HEADS * (MLA_NOPE_DIM + MLA_ROPE_DIM)), f32) * MLA_Q_RANK ** -0.5
    mla_kv_norm = 1.0 + 0.02 * jax.random.normal(ks[7], (DEPTH, MLA_KV_RANK), f32)
    w_ukv = jax.random.normal(ks[8], (DEPTH, MLA_KV_RANK, N_HEADS * (MLA_NOPE_DIM + MLA_V_DIM)), f32) * MLA_KV_RANK ** -0.5
    swa_sinks = jax.random.normal(ks[9], (DEPTH, N_HEADS), f32) * 0.5
    w_mem_kv = jax.random.normal(ks[10], (DEPTH, D_MODEL, 2 * MEM_HEADS * HEAD_DIM), f32) * D_MODEL ** -0.5
    w_out = jax.random.normal(ks[11], (DEPTH, D_MODEL, D_MODEL), f32) * (D_MODEL ** -0.5) * beta
    ln_gain = 1.0 + 0.02 * jax.random.normal(ks[12], (DEPTH, D_MODEL), f32)
    ln_bias = 0.02 * jax.random.normal(ks[13], (DEPTH, D_MODEL), f32)
    return {'x': x, 'mem': mem, 'positions': positions, 'w_in': w_in, 'rel_bias': rel_bias,
            'mla_q_norm': mla_q_norm, 'w_uq': w_uq, 'mla_kv_norm': mla_kv_norm, 'w_ukv': w_ukv,
            'swa_sinks': swa_sinks, 'w_mem_kv': w_mem_kv, 'w_out': w_out,
            'ln_gain': ln_gain, 'ln_bias': ln_bias}


def reference(x, mem, positions, w_in, rel_bias, mla_q_norm, w_uq, mla_kv_norm, w_ukv,
              swa_sinks, w_mem_kv, w_out, ln_gain, ln_bias):
    h = x
    for l in range(DEPTH):
        h = _layer(h, mem, positions, w_in[l], rel_bias[l], mla_q_norm[l], w_uq[l],
                   mla_kv_norm[l], w_ukv[l], swa_sinks[l], w_mem_kv[l], w_out[l],
                   ln_gain[l], ln_bias[l])
    return h
```

```python
import math
from contextlib import ExitStack
import numpy as np
import concourse.bass as bass
import concourse.mybir as mybir
from concourse.bass_utils import run_bass_kernel_spmd

F32 = mybir.dt.float32
BF16 = mybir.dt.bfloat16
I32 = mybir.dt.int32
ALU = mybir.AluOpType
AF = mybir.ActivationFunctionType

NL = 4
NSEQ = 2
S = 4096
D = 1024
T = S // 128
NCORES = 8
PW = 2912
ALPHA = (2.0 * NL) ** 0.25
TWO_PI = 2.0 * math.pi
C1 = float(np.float32(6.28125))
C2 = float(np.float32(TWO_PI - 6.28125))
PI_LO = float(np.nextafter(np.float32(math.pi), np.float32(0)))
SCALE_B = 96.0 ** -0.5

FM0 = 0
G0 = 768
G1 = 1280
T2 = 1792
T3 = 2304
T4 = 2784


class _Op:
    __slots__ = ("eng", "fn", "deps", "dma", "eidx", "sem", "val", "need")

    def __init__(self, eng, fn, deps, dma):
        self.eng, self.fn, self.deps, self.dma = eng, fn, deps, dma
        self.eidx = 0
        self.sem = None
        self.val = 0
        self.need = False


class Tracker:
    EPOCH = 30000
    NDMA = 20
    NEAR = 1 << 30
    LIMIT = None

    def __init__(self, nc, es):
        self.nc = nc
        self.es = es
        self.engs = {"pe": nc.tensor, "act": nc.scalar, "dve": nc.vector, "pool": nc.gpsimd, "sp": nc.sync}
        self.ops = []
        self.last_w = {}
        self.readers = {}
        self.xlast = {}
        self.rec = None

    def begin(self):
        self.rec = []

    def end(self):
        lst, self.rec = self.rec, None
        return lst

    def commit(self, lst):
        for a in lst:
            self._op(*a)

    def op(self, eng, fn, r=(), w=(), dma=False):
        if self.rec is not None:
            self.rec.append((eng, fn, tuple(r), tuple(w), dma))
            return
        self._op(eng, fn, r, w, dma)

    def _op(self, eng, fn, r=(), w=(), dma=False):
        i = len(self.ops)
        deps = set()
        for k in r:
            j = self.last_w.get(k)
            if j is not None:
                deps.add(j)
            if isinstance(k, tuple) and k[0] == "ps":
                xl = self.xlast.setdefault(k, {})
                for e2, j2 in xl.items():
                    if e2 != eng:
                        deps.add(j2)
                xl[eng] = i
        for k in w:
            j = self.last_w.get(k)
            if j is not None:
                deps.add(j)
            for j in self.readers.get(k, ()):
                deps.add(j)
        for k in r:
            self.readers.setdefault(k, []).append(i)
        for k in w:
            self.last_w[k] = i
            self.readers[k] = []
            self.xlast[k] = {}
        deps.discard(i)
        self.ops.append(_Op(eng, fn, deps, dma))

    def emit(self):
        nc, ops = self.nc, self.ops
        if self.LIMIT is not None:
            ops = self.ops = self.ops[:self.LIMIT]
            for o in ops:
                o.deps = {j for j in o.deps if j < len(ops)}
        cnt = {e: 0 for e in self.engs}
        for o in ops:
            o.eidx = cnt[o.eng]
            cnt[o.eng] += 1
        for o in ops:
            for j in o.deps:
                d = ops[j]
                if d.dma:
                    continue
                if d.eng != o.eng or o.dma:
                    d.need = True
                elif o.eng != "pe" and o.eidx - d.eidx <= self.NEAR:
                    d.need = True
        sem_ctr = [0]

        def new_sem():
            sem_ctr[0] += 1
            return self.es.enter_context(nc.semaphore(f"sm{sem_ctr[0]}"))

        cur_sem = {e: new_sem() for e in self.engs}
        cur_cnt = {e: 0 for e in self.engs}
        dma_sems = [new_sem() for _ in range(self.NDMA)]
        dma_tot = [0] * self.NDMA
        half = self.NDMA // 2
        dma_rr = {"sp": 0, "pool": 0, "act": 0}
        dma_base = {"sp": 0, "pool": half, "act": 0}
        observed = {e: {} for e in self.engs}

        def wait(e, sem, val):
            ob = observed[e]
            k = id(sem)
            if ob.get(k, 0) >= val:
                return
            self.engs[e].wait_ge(sem, val)
            ob[k] = val

        for o in ops:
            e = o.eng
            for j in sorted(o.deps):
                d = ops[j]
                if d.dma or d.need:
                    if (not d.dma) and d.eng == e and not o.dma and not (e != "pe" and o.eidx - d.eidx <= self.NEAR):
                        continue
                    wait(e, d.sem, d.val)
            if o.dma:
                k = dma_base[e] + dma_rr[e]
                dma_rr[e] = (dma_rr[e] + 1) % half
                if dma_tot[k] > 0:
                    wait(e, dma_sems[k], dma_tot[k])
                ins = o.fn()
                dma_tot[k] += 16
                ins.then_inc(dma_sems[k], 16)
                o.sem, o.val = dma_sems[k], dma_tot[k]
            else:
                ins = o.fn()
                if o.need:
                    if cur_cnt[e] >= self.EPOCH:
                        cur_sem[e] = new_sem()
                        cur_cnt[e] = 0
                    cur_cnt[e] += 1
                    ins.then_inc(cur_sem[e], 1)
                    o.sem, o.val = cur_sem[e], cur_cnt[e]
        for k in range(self.NDMA):
            if dma_tot[k] > 0:
                wait("sp", dma_sems[k], dma_tot[k])


def build_program(nl=NL, nseq=NSEQ, ntiles=T):
    nc = bass.Bass("TRN2", target_bir_lowering=False)
    es = ExitStack()
    dt = nc.dram_tensor
    x_d = dt("x", [NSEQ, S, D], F32, kind="ExternalInput").ap()
    mem_d = dt("mem", [NSEQ, 256, D], F32, kind="ExternalInput").ap()
    pos_d = dt("pos", [NSEQ, 128, T], I32, kind="ExternalInput").ap()
    win_d = dt("w_in", [NL, D, PW], F32, kind="ExternalInput").ap()
    tbl_d = dt("relb", [NL, 128, 3, 512], F32, kind="ExternalInput").ap()
    qn_d = dt("qn", [NL, 192], F32, kind="ExternalInput").ap()
    wuq_d = dt("w_uq", [NL, 192, 384], F32, kind="ExternalInput").ap()
    kvn_d = dt("kvn", [NL, 128], F32, kind="ExternalInput").ap()
    wukv_d = dt("w_ukv", [NL, 128, 512], F32, kind="ExternalInput").ap()
    sink_d = dt("sinks", [NL, 4], F32, kind="ExternalInput").ap()
    wmk_d = dt("w_mem_kv", [NL, D, 512], F32, kind="ExternalInput").ap()
    wout_d = dt("w_out", [NL, D, D], F32, kind="ExternalInput").ap()
    lng_d = dt("ln_gain", [NL, D], F32, kind="ExternalInput").ap()
    lnb_d = dt("ln_bias", [NL, D], F32, kind="ExternalInput").ap()
    idn_d = dt("ident", [128, 128], F32, kind="ExternalInput").ap()
    ifc_d = dt("invf", [128, 48], F32, kind="ExternalInput").ap()
    out_d = dt("out", [NSEQ, S, D], F32, kind="ExternalOutput").ap()
    hb_d = dt("hbuf", [2, S, D], F32, kind="Internal").ap()

    def sb(name, shape, dtype):
        return es.enter_context(nc.sbuf_tensor(name, shape, dtype))

    w_in_sb = sb("w_in_sb", [128, 8, PW], BF16)
    w_out_sb = sb("w_out_sb", [128, 8, D], BF16)
    w_uq_sb = sb("w_uq_sb", [128, 2, 384], BF16)
    w_ukv_sb = sb("w_ukv_sb", [128, 512], BF16)
    tblh = sb("tblh", [128, 3, 512], BF16)
    tbll = sb("tbll", [128, 3, 512], BF16)
    G_sb = sb("G_sb", [128, D], F32)
    B_sb = sb("B_sb", [128, D], F32)
    gq_sb = sb("gq_sb", [128, 2], F32)
    gkv_sb = sb("gkv_sb", [128, 1], F32)
    esink = sb("esink", [128, 4], F32)
    KT_B = sb("KT_B", [128, 4, S], BF16)
    V_B = sb("V_B", [128, T, 4, 65], BF16)
    KT_A = sb("KT_A", [128, 2, 8, 128], BF16)
    V_A = sb("V_A", [128, 8, 4, 65], BF16)
    KT_C = sb("KT_C", [128, 4, 128], BF16)
    V_C = sb("V_C", [128, 4, 2, 65], BF16)
    memT = sb("memT", [128, 8, 256], BF16)
    KmT = sb("KmT", [128, 2, 256], BF16)
    VmA = sb("VmA", [128, 2, 4, 65], BF16)
    cosC = sb("cosC", [128, T, 32], F32)
    sinC = sb("sinC", [128, T, 32], F32)
    cosB = sb("cosB", [128, T, 16], F32)
    sinB = sb("sinB", [128, T, 16], F32)
    ident = sb("ident_sb", [128, 128], BF16)
    zeros = sb("zeros", [128, 128], BF16)
    mhalf = sb("mhalf", [128, 1], F32)
    invf = sb("invf_sb", [128, 48], F32)
    posi = sb("posi", [128, T], I32)
    posf = sb("posf", [128, T], F32)
    xb = [sb(f"xb{i}", [128, D], BF16) for i in range(2)]
    xT = sb("xT", [128, 8, 128], BF16)
    zbuf = sb("zbuf", [128, 2, D], F32)
    z = [zbuf[:, i, :] for i in range(2)]
    wmk_sb = zbuf[:].rearrange("p a d -> p (a d)").bitcast(BF16).rearrange("p (k c) -> p k c", k=8)
    sg_l = [sb(f"sg{i}", [128, D], F32) for i in range(2)]
    aqT_l = [sb(f"aqT{i}", [128, 4, 128], BF16) for i in range(2)]
    mqT_l = [sb(f"mqT{i}", [128, 4, 128], BF16) for i in range(2)]
    cqT_l = [sb(f"cqT{i}", [128, 4, 128], BF16) for i in range(2)]
    QT_l = [sb(f"QT{i}", [128, 4, 128], BF16) for i in range(2)]
    sg = sg_l[0]
    Q_tok = sb("Q_tok", [128, 4, 96], BF16)
    K_tok = sb("K_tok", [128, 4, 96], BF16)
    cqk_tok = sb("cqk_tok", [128, 6, 2, 32], BF16)
    rt = [sb(f"rt{i}", [128, 384], F32) for i in range(2)]
    kr = sb("kr", [128, 32], F32)
    cqn = sb("cqn", [128, 192], BF16)
    ckvn = sb("ckvn", [128, 128], BF16)
    cqnT = sb("cqnT", [128, 2, 128], BF16)
    ckvnT = sb("ckvnT", [128, 128], BF16)
    st6 = sb("st6", [128, 4, 6], F32)
    mv = sb("mv", [128, 3, 2], F32)
    sm = sb("sm", [128, 8], F32)
    NPT = 3
    PT = [sb(f"PT{i}", [128, 512], BF16) for i in range(NPT)]
    PT_lo = [sb(f"PTlo{i}", [128, 512], BF16) for i in range(1)]
    PT_hi = [sb(f"PThi{i}", [128, 512], BF16) for i in range(3)]
    den = sb("den", [128, 16], F32)
    rec = sb("rec", [128, 16], F32)
    att = sb("att", [128, D], BF16)
    gT = sb("gT", [128, 8, 128], BF16)

    ps = [es.enter_context(nc.psum_tensor(f"ps{i}", [128, 512], F32)) for i in range(8)]
    psb = [p[:].bitcast(BF16) for p in ps]

    tr = Tracker(nc, es)
    WINK = [("w_in", k, c0) for k in range(8) for c0 in (0, 1456)]
    PE, ACT, DVE, POOL, SP = "pe", "act", "dve", "pool", "sp"
    rotA = [0]
    rotB = [0]

    def bankA():
        i = 6 + rotA[0] % 2
        rotA[0] += 1
        return i

    def bankB():
        i = 3 + rotB[0] % 2
        rotB[0] += 1
        return i

    def bankC():
        return 5

    bank = bankB

    def bk(i):
        return ("ps", i)

    tr.op(POOL, lambda: nc.gpsimd.dma_start(out=ident[:], in_=idn_d[:, :]), w=["ident"], dma=True)
    tr.op(SP, lambda: nc.sync.dma_start(out=invf[:], in_=ifc_d[:, :]), w=["invf"], dma=True)
    tr.op(POOL, lambda: nc.gpsimd.memset(zeros[:], 0.0), w=["zeros"])
    tr.op(POOL, lambda: nc.gpsimd.memset(mhalf[:], -0.5), w=["mhalf"])
    for i in range(1):
        tr.op(POOL, lambda i=i: nc.gpsimd.memset(PT_lo[i][:], 0.0), w=[("PTlo", i)])
    for i in range(3):
        tr.op(POOL, lambda i=i: nc.gpsimd.memset(PT_hi[i][:], 0.0), w=[("PThi", i)])
    tr.op(POOL, lambda: nc.gpsimd.memset(V_A[:], 1.0), w=[("V_A", i) for i in range(8)])
    tr.op(POOL, lambda: nc.gpsimd.memset(V_C[:], 1.0), w=[("V_C", i) for i in range(4)])
    tr.op(POOL, lambda: nc.gpsimd.memset(V_B[:], 1.0), w=[("V_B", i) for i in range(T)])
    tr.op(POOL, lambda: nc.gpsimd.memset(VmA[:], 1.0), w=["VmA"])
    tr.op(POOL, lambda: nc.gpsimd.memset(cqnT[:], 0.0), w=["cqnT"])
    for i in range(2):
        tr.op(POOL, lambda i=i: nc.gpsimd.memset(aqT_l[i][:], 0.0), w=[("aqT", i)])
        tr.op(POOL, lambda i=i: nc.gpsimd.memset(mqT_l[i][:], 0.0), w=[("mqT", i)])
        tr.op(POOL, lambda i=i: nc.gpsimd.memset(cqT_l[i][:], 0.0), w=[("cqT", i)])
    tr.op(POOL, lambda: nc.gpsimd.memset(w_uq_sb[:], 0.0), w=["w_uq"])

    SGK = [("sg", 0, 0), ("sg", 0, 1)]
    lo_rr = [0]
    hi_rr = [0]
    pt_rr = [0]
    tmp_rr = [0]

    def rsqrt_chain(src_key, v_ap, eps, out_ap, out_key):
        tr.op(DVE, lambda: nc.vector.tensor_scalar(out=v_ap, in0=v_ap, scalar1=eps, scalar2=None, op0=ALU.add),
              r=[src_key], w=[src_key])
        tr.op(POOL, lambda: nc.gpsimd.tensor_tensor(out_ap, v_ap, mhalf[:], ALU.pow),
              r=[src_key, "mhalf"], w=[out_key])

    def rope(xv, nh, f, cos_ap, sin_ap, ov, src_keys, dst_keys, scale_ap=None, scale_key=None):
        n = nh * 2 * f
        t1 = rt[0][:, 0:n].rearrange("p (h two f) -> p h two f", h=nh, two=2)
        t2 = rt[1][:, 0:n].rearrange("p (h two f) -> p h two f", h=nh, two=2)
        cb = cos_ap.unsqueeze(1).unsqueeze(1).broadcast_to([128, nh, 2, f])
        sbb = sin_ap.unsqueeze(1).broadcast_to([128, nh, f])
        sk = list(src_keys) + ([scale_key] if scale_key is not None else [])

        def mul(o_, a_, b_):
            if scale_ap is None:
                return lambda: nc.vector.tensor_tensor(o_, a_, b_, ALU.mult)
            return lambda: nc.vector.scalar_tensor_tensor(out=o_, in0=a_, scalar=scale_ap, in1=b_, op0=ALU.mult, op1=ALU.mult)
        if scale_ap is None:
            tr.op(DVE, mul(t1, xv, cb), r=sk, w=[("rt", 0)])
        else:
            cb3 = cos_ap.unsqueeze(1).broadcast_to([128, nh, f])
            tr.op(DVE, mul(t1[:, :, 0, :], xv[:, :, 0, :], cb3), r=sk, w=[("rt", 0)])
            tr.op(DVE, mul(t1[:, :, 1, :], xv[:, :, 1, :], cb3), r=sk, w=[("rt", 0)])
        tr.op(DVE, mul(t2[:, :, 0, :], xv[:, :, 1, :], sbb), r=sk, w=[("rt", 1)])
        tr.op(DVE, mul(t2[:, :, 1, :], xv[:, :, 0, :], sbb), r=sk, w=[("rt", 1)])
        tr.op(DVE, lambda: nc.vector.tensor_tensor(ov[:, :, 0, :], t1[:, :, 0, :], t2[:, :, 0, :], ALU.subtract),
              r=[("rt", 0), ("rt", 1)], w=dst_keys)
        tr.op(DVE, lambda: nc.vector.tensor_tensor(ov[:, :, 1, :], t1[:, :, 1, :], t2[:, :, 1, :], ALU.add),
              r=[("rt", 0), ("rt", 1)], w=dst_keys)

    def mm_group(out_ap, pairs, okey, rkeys):
        def fn():
            n = len(pairs)
            ins = None
            for i, (l, r_) in enumerate(pairs):
                ins = nc.tensor.matmul(out_ap, l, r_, start=(i == 0), stop=(i == n - 1))
            return ins
        tr.op(PE, fn, r=rkeys, w=[okey])

    def transposes(items, okey, rkeys):
        def fn():
            ins = None
            for (o_, i_) in items:
                ins = nc.tensor.transpose(o_, i_, ident[:])
            return ins
        tr.op(PE, fn, r=rkeys + ["ident"], w=[okey])

    def seq_setup(s):
        tr.op(SP, lambda: nc.sync.dma_start(out=posi[:], in_=pos_d[s, :, :]), w=["posi"], dma=True)
        tr.op(DVE, lambda: nc.vector.tensor_copy(posf[:], posi[:]), r=["posi"], w=["posf"])
        for (cos_t, sin_t, f0, nf, ck, sk) in ((cosC, sinC, 0, 32, "cosC", "sinC"), (cosB, sinB, 32, 16, "cosB", "sinB")):
            n = T * nf
            ang = sg[:, 0:n]
            kf = z[0][:, 0:n]
            ki = z[1][:, 0:n].bitcast(I32)
            r2 = z[1][:, 0:n]
            ang3 = ang.rearrange("p (t f) -> p t f", f=nf)

            def f_ang(ang3=ang3, f0=f0, nf=nf):
                ins = None
                for t in range(T):
                    ins = nc.vector.tensor_scalar(out=ang3[:, t, :], in0=invf[:, f0:f0 + nf], scalar1=posf[:, t:t + 1],
                                                  scalar2=None, op0=ALU.mult)
                return ins
            tr.op(DVE, f_ang, r=["posf", "invf"], w=SGK)
            tr.op(DVE, lambda kf=kf, ang=ang: nc.vector.tensor_scalar(out=kf, in0=ang, scalar1=1.0 / TWO_PI, scalar2=None, op0=ALU.mult),
                  r=SGK, w=[("z", 0)])
            tr.op(DVE, lambda ki=ki, kf=kf: nc.vector.tensor_copy(ki, kf), r=[("z", 0)], w=[("z", 1)])
            tr.op(DVE, lambda ki=ki, kf=kf: nc.vector.tensor_copy(kf, ki), r=[("z", 1)], w=[("z", 0)])
            tr.op(DVE, lambda r2=r2, kf=kf, ang=ang: nc.vector.scalar_tensor_tensor(out=r2, in0=kf, scalar=-C1, in1=ang, op0=ALU.mult, op1=ALU.add),
                  r=[("z", 0)] + SGK, w=[("z", 1)])
            tr.op(DVE, lambda r2=r2, kf=kf: nc.vector.scalar_tensor_tensor(out=r2, in0=kf, scalar=-C2, in1=r2, op0=ALU.mult, op1=ALU.add),
                  r=[("z", 0), ("z", 1)], w=[("z", 1)])

            def wrap(buf, key):
                key = key if isinstance(key, list) else [key]
                m = z[0][:, 0:n]
                tr.op(DVE, lambda: nc.vector.tensor_single_scalar(out=m, in_=buf, scalar=math.pi, op=ALU.is_gt), r=key, w=[("z", 0)])
                tr.op(DVE, lambda: nc.vector.scalar_tensor_tensor(out=buf, in0=m, scalar=-TWO_PI, in1=buf, op0=ALU.mult, op1=ALU.add),
                      r=[("z", 0)] + key, w=key)
                tr.op(DVE, lambda: nc.vector.tensor_single_scalar(out=m, in_=buf, scalar=-math.pi, op=ALU.is_lt), r=key, w=[("z", 0)])
                tr.op(DVE, lambda: nc.vector.scalar_tensor_tensor(out=buf, in0=m, scalar=TWO_PI, in1=buf, op0=ALU.mult, op1=ALU.add),
                      r=[("z", 0)] + key, w=key)
                tr.op(DVE, lambda: nc.vector.tensor_scalar(out=buf, in0=buf, scalar1=PI_LO, scalar2=-PI_LO, op0=ALU.min, op1=ALU.max),
                      r=key, w=key)
            wrap(r2, ("z", 1))
            tr.op(ACT, lambda sin_t=sin_t, r2=r2: nc.scalar.activation(out=sin_t[:].rearrange("p t f -> p (t f)"), in_=r2, func=AF.Sin),
                  r=[("z", 1)], w=[sk])
            c2 = sg[:, 0:n]
            tr.op(DVE, lambda c2=c2, r2=r2: nc.vector.tensor_scalar(out=c2, in0=r2, scalar1=math.pi / 2, scalar2=None, op0=ALU.add),
                  r=[("z", 1)], w=SGK)
            wrap(c2, SGK)
            tr.op(ACT, lambda cos_t=cos_t, c2=c2: nc.scalar.activation(out=cos_t[:].rearrange("p t f -> p (t f)"), in_=c2, func=AF.Sin),
                  r=SGK, w=[ck])
        for mb in range(2):
            tr.op(POOL, lambda mb=mb: nc.gpsimd.dma_start(out=xb[mb][:], in_=mem_d[s, mb * 128:(mb + 1) * 128, :]),
                  w=[("xb", mb)], dma=True)
            b = bank()
            transposes([(psb[b][:, k * 128:(k + 1) * 128], xb[mb][:, k * 128:(k + 1) * 128]) for k in range(8)],
                       bk(b), [("xb", mb)])
            tr.op(ACT, lambda b=b, mb=mb: nc.scalar.copy(out=memT[:, :, mb * 128:(mb + 1) * 128],
                                                        in_=psb[b][:, 0:1024].rearrange("p (k c) -> p k c", k=8)),
                  r=[bk(b)], w=["memT"])

    def layer_setup_early(l):
        for k in range(8):
            for (c0, c1) in ((0, 1456), (1456, PW)):
                tr.op(POOL, lambda k=k, c0=c0, c1=c1: nc.gpsimd.dma_start(out=w_in_sb[:, k, c0:c1], in_=win_d[l, k * 128:(k + 1) * 128, c0:c1]),
                      w=[("w_in", k, c0)], dma=True)
        tr.op(POOL, lambda: nc.gpsimd.dma_start(out=w_uq_sb[:, 0, :], in_=wuq_d[l, 0:128, :]), w=["w_uq"], dma=True)
        tr.op(POOL, lambda: nc.gpsimd.dma_start(out=w_uq_sb[0:64, 1, :], in_=wuq_d[l, 128:192, :]), w=["w_uq"], dma=True)
        tr.op(POOL, lambda: nc.gpsimd.dma_start(out=w_ukv_sb[:, :], in_=wukv_d[l, :, :]), w=["w_ukv"], dma=True)
        tr.op(SP, lambda: nc.sync.dma_start(out=gq_sb[:, 0:1], in_=qn_d[l, 0:128].rearrange("(p o) -> p o", o=1)), w=["gq"], dma=True)
        tr.op(SP, lambda: nc.sync.dma_start(out=gq_sb[0:64, 1:2], in_=qn_d[l, 128:192].rearrange("(p o) -> p o", o=1)), w=["gq"], dma=True)
        tr.op(SP, lambda: nc.sync.dma_start(out=gkv_sb[:, 0:1], in_=kvn_d[l, :].rearrange("(p o) -> p o", o=1)), w=["gkv"], dma=True)

    def layer_setup_late(l):
        for k in range(8):
            tr.op(POOL, lambda k=k: nc.gpsimd.dma_start(out=wmk_sb[:, k, :], in_=wmk_d[l, k * 128:(k + 1) * 128, :]),
                  w=[("wmk", k), ("z", 0), ("z", 1)], dma=True)
        for k in range(8):
            tr.op(POOL, lambda k=k: nc.gpsimd.dma_start(out=w_out_sb[:, k, :], in_=wout_d[l, k * 128:(k + 1) * 128, :]),
                  w=[("w_out", k)], dma=True)
        for i in range(3):
            stg = sg_l[i // 2][:, (i % 2) * 512:(i % 2 + 1) * 512]
            sk = ("sg", i // 2, i % 2)
            tr.op(SP, lambda i=i, stg=stg: nc.sync.dma_start(out=stg, in_=tbl_d[l, :, i, :]), w=[sk], dma=True)
            tr.op(POOL, lambda stg=stg: nc.gpsimd.tensor_scalar(out=stg, in0=stg, scalar1=8.0, scalar2=None, op0=ALU.mult), r=[sk], w=[sk])
            tr.op(POOL, lambda i=i, stg=stg: nc.gpsimd.tensor_copy(tblh[:, i, :], stg), r=[sk], w=[("tblh", i)])
            tr.op(POOL, lambda i=i, stg=stg: nc.gpsimd.tensor_tensor(stg, stg, tblh[:, i, :], ALU.subtract), r=[sk, ("tblh", i)], w=[sk])
            tr.op(POOL, lambda i=i, stg=stg: nc.gpsimd.tensor_copy(tbll[:, i, :], stg), r=[sk], w=[("tbll", i)])
        tr.op(SP, lambda: nc.sync.dma_start(out=G_sb[:], in_=lng_d[l, :].partition_broadcast(128)), w=["G"], dma=True)
        tr.op(SP, lambda: nc.sync.dma_start(out=B_sb[:], in_=lnb_d[l, :].partition_broadcast(128)), w=["Bv"], dma=True)
        tr.op(SP, lambda: nc.sync.dma_start(out=esink[:], in_=sink_d[l, :].partition_broadcast(128)), w=["esink"], dma=True)
        tr.op(ACT, lambda: nc.scalar.activation(out=esink[:], in_=esink[:], func=AF.Exp), r=["esink"], w=["esink"])

    def mem_kv():
        for pair in range(2):
            b = bank()
            mm_group(ps[b][:, 0:256], [(wmk_sb[:, k, pair * 128:(pair + 1) * 128], memT[:, k, :]) for k in range(8)],
                     bk(b), ["memT", ("z", 0), ("z", 1)] + [("wmk", k) for k in range(8)])
            tr.op(ACT, lambda b=b, pair=pair: nc.scalar.copy(out=KmT[:, pair, :], in_=ps[b][:, 0:256]), r=[bk(b)], w=["KmT"])
        for mb in range(2):
            b = bank()
            mm_group(ps[b][:, 0:256], [(memT[:, k, mb * 128:(mb + 1) * 128], wmk_sb[:, k, 256:512]) for k in range(8)],
                     bk(b), ["memT", ("z", 0), ("z", 1)] + [("wmk", k) for k in range(8)])
            tr.op(ACT, lambda b=b, mb=mb: nc.scalar.copy(out=VmA[:, mb, :, 0:64], in_=ps[b][:, 0:256].rearrange("p (h d) -> p h d", h=4)),
                  r=[bk(b)], w=["VmA"])

    def issue_loads(l, t, src, skey, slot, par):
        tr.op(POOL, lambda: nc.gpsimd.dma_start(out=xb[slot][:], in_=src[t * 128:(t + 1) * 128, :]),
              r=[(skey, t)], w=[("xb", slot)], dma=True)
        tr.op(SP, lambda: nc.sync.dma_start(out=z[par][:], in_=src[t * 128:(t + 1) * 128, :]),
              r=[(skey, t)], w=[("z", par)], dma=True)

    def xT_step(slot):
        b = bankA()
        transposes([(psb[b][:, k * 128:(k + 1) * 128], xb[slot][:, k * 128:(k + 1) * 128]) for k in range(8)],
                   bk(b), [("xb", slot)])
        tr.op(ACT, lambda b=b: nc.scalar.copy(out=xT[:].rearrange("p k c -> p (k c)"), in_=psb[b][:, 0:1024]), r=[bk(b)], w=["xT"])

    def stageA(l, t, slot, par):
        sA = t % 8
        sC = t % 4
        aqT, mqT, cqT, QT, sg = aqT_l[par], mqT_l[par], cqT_l[par], QT_l[par], sg_l[par]

        def tm(c0, n):
            b_ = bankA()
            mm_group(ps[b_][:, 0:n], [(xT[:, k, :], w_in_sb[:, k, c0:c0 + n]) for k in range(8)], bk(b_), ["xT"] + WINK)
            return b_

        def fm(c0, nchunks):
            b_ = bankA()
            for j in range(nchunks):
                mm_group(ps[b_][:, j * 128:(j + 1) * 128],
                         [(w_in_sb[:, k, c0 + j * 128:c0 + (j + 1) * 128], xT[:, k, :]) for k in range(8)],
                         bk(b_), ["xT"] + WINK)
            return b_

        b3 = tm(T3, 480)
        b4 = tm(T4, 128)
        tr.op(DVE, lambda: nc.vector.bn_stats(st6[:, 0, :], ps[b3][:, 256:448]), r=[bk(b3)], w=["st6a"])
        tr.op(DVE, lambda: nc.vector.bn_aggr(mv[:, 0, :], st6[:, 0, :]), r=["st6a"], w=["mv0"])
        tr.op(DVE, lambda: nc.vector.scalar_tensor_tensor(out=sm[:, 0:1], in0=mv[:, 0, 0:1], scalar=mv[:, 0, 0:1], in1=mv[:, 0, 1:2],
                                                         op0=ALU.mult, op1=ALU.add), r=["mv0"], w=["sm0"])
        rsqrt_chain("sm0", sm[:, 0:1], 1e-6, sm[:, 1:2], "sm1")
        tr.op(DVE, lambda: nc.vector.bn_stats(st6[:, 1, :], ps[b4][:, 0:128]), r=[bk(b4)], w=["st6b"])
        tr.op(DVE, lambda: nc.vector.bn_aggr(mv[:, 1, :], st6[:, 1, :]), r=["st6b"], w=["mv1"])
        tr.op(DVE, lambda: nc.vector.scalar_tensor_tensor(out=sm[:, 2:3], in0=mv[:, 1, 0:1], scalar=mv[:, 1, 0:1], in1=mv[:, 1, 1:2],
                                                         op0=ALU.mult, op1=ALU.add), r=["mv1"], w=["sm2"])
        rsqrt_chain("sm2", sm[:, 2:3], 1e-6, sm[:, 3:4], "sm3")
        tr.op(ACT, lambda: nc.scalar.copy(out=cqn[:], in_=ps[b3][:, 256:448]), r=[bk(b3)], w=["cqn"])
        tr.op(ACT, lambda: nc.scalar.copy(out=ckvn[:], in_=ps[b4][:, 0:128]), r=[bk(b4)], w=["ckvn"])
        tr.op(ACT, lambda: nc.scalar.copy(out=V_A[:, sA, :, 0:64], in_=ps[b3][:, 0:256].rearrange("p (h d) -> p h d", h=4)),
              r=[bk(b3)], w=[("V_A", sA)])
        rope(ps[b3][:, 448:480].rearrange("p (o two f) -> p o two f", o=1, two=2), 1, 16, cosB[:, t, :], sinB[:, t, :],
             kr[:, 0:32].rearrange("p (o two f) -> p o two f", o=1, two=2), [bk(b3), "cosB", "sinB"], ["kr"])
        tr.op(DVE, lambda: nc.vector.tensor_copy(K_tok[:, :, 64:96], kr[:].unsqueeze(1).broadcast_to([128, 4, 32])),
              r=["kr"], w=["K_tok"])
        b2 = tm(T2, 512)
        v6 = ps[b2][:, 0:384].rearrange("p (h two f) -> p h two f", h=6, two=2)
        rope(v6, 6, 32, cosC[:, t, :], sinC[:, t, :], cqk_tok[:, :, :, :], [bk(b2), "cosC", "sinC"], ["cqk_tok"])
        tr.op(ACT, lambda: nc.scalar.copy(out=V_C[:, sC, :, 0:64], in_=ps[b2][:, 384:512].rearrange("p (h d) -> p h d", h=2)),
              r=[bk(b2)], w=[("V_C", sC)])
        bA = fm(FM0, 4)
        tr.op(DVE, lambda: nc.vector.tensor_copy(aqT[:].rearrange("p (a two) c -> p a two c", two=2)[0:64, :, 0, :],
                                                 ps[bA][0:64, 0:256].rearrange("p (a c) -> p a c", a=2)), r=[bk(bA)], w=[("aqT", par)])
        tr.op(DVE, lambda: nc.vector.tensor_copy(aqT[:].rearrange("p (a two) c -> p a two c", two=2)[64:128, :, 1, :],
                                                 ps[bA][64:128, 0:256].rearrange("p (a c) -> p a c", a=2)), r=[bk(bA)], w=[("aqT", par)])
        tr.op(DVE, lambda: nc.vector.tensor_copy(KT_A[:, :, sA, :], ps[bA][:, 256:512].rearrange("p (a c) -> p a c", a=2)),
              r=[bk(bA)], w=[("KT_A", sA)])
        bM = fm(FM0 + 512, 2)
        tr.op(DVE, lambda: nc.vector.tensor_copy(mqT[:].rearrange("p (a two) c -> p a two c", two=2)[0:64, :, 0, :],
                                                 ps[bM][0:64, 0:256].rearrange("p (a c) -> p a c", a=2)), r=[bk(bM)], w=[("mqT", par)])
        tr.op(DVE, lambda: nc.vector.tensor_copy(mqT[:].rearrange("p (a two) c -> p a two c", two=2)[64:128, :, 1, :],
                                                 ps[bM][64:128, 0:256].rearrange("p (a c) -> p a c", a=2)), r=[bk(bM)], w=[("mqT", par)])
        for gi, c0 in enumerate((G0, G1)):
            bg = tm(c0, 512)
            tr.op(ACT, lambda bg=bg, gi=gi: nc.scalar.activation(out=sg[:, gi * 512:(gi + 1) * 512], in_=ps[bg][:, :], func=AF.Tanh, scale=0.5),
                  r=[bk(bg)], w=[("sg", par, gi)])
            tr.op(DVE, lambda bg=bg, gi=gi: nc.vector.scalar_tensor_tensor(out=sg[:, gi * 512:(gi + 1) * 512], in0=sg[:, gi * 512:(gi + 1) * 512],
                                                                          scalar=1.0, in1=ps[bg][:, :], op0=ALU.add, op1=ALU.mult),
                  r=[bk(bg), ("sg", par, gi)], w=[("sg", par, gi)])
        bt = bankA()
        transposes([(psb[bt][:, 0:128], cqn[:, 0:128]), (psb[bt][0:64, 128:256], cqn[:, 128:192]), (psb[bt][:, 256:384], ckvn[:, :])],
                   bk(bt), ["cqn", "ckvn"])
        tr.op(DVE, lambda: nc.vector.tensor_scalar(out=cqnT[:, 0, :], in0=psb[bt][:, 0:128], scalar1=gq_sb[:, 0:1], scalar2=None, op0=ALU.mult),
              r=[bk(bt), "gq"], w=["cqnT"])
        tr.op(DVE, lambda: nc.vector.tensor_scalar(out=cqnT[0:64, 1, :], in0=psb[bt][0:64, 128:256], scalar1=gq_sb[0:64, 1:2], scalar2=None, op0=ALU.mult),
              r=[bk(bt), "gq"], w=["cqnT"])
        tr.op(DVE, lambda: nc.vector.tensor_scalar(out=ckvnT[:, :], in0=psb[bt][:, 256:384], scalar1=gkv_sb[:, 0:1], scalar2=None, op0=ALU.mult),
              r=[bk(bt), "gkv"], w=["ckvnT"])
        bq = bankA()
        mm_group(ps[bq][:, 0:384], [(cqnT[:, 0, :], w_uq_sb[:, 0, :]), (cqnT[:, 1, :], w_uq_sb[:, 1, :])], bk(bq), ["cqnT", "w_uq"])
        bkv = bankA()
        mm_group(ps[bkv][:, 0:512], [(ckvnT[:, :], w_ukv_sb[:, :])], bk(bkv), ["ckvnT", "w_ukv"])
        q4 = ps[bq][:, 0:384].rearrange("p (h c) -> p h c", h=4)
        tr.op(ACT, lambda: nc.scalar.activation(out=Q_tok[:, :, 0:64], in_=q4[:, :, 0:64], func=AF.Copy, scale=sm[:, 1:2]), r=[bk(bq), "sm1"], w=["Q_tok"])
        rope(q4[:, :, 64:96].rearrange("p h (two f) -> p h two f", two=2), 4, 16, cosB[:, t, :], sinB[:, t, :],
             Q_tok[:, :, 64:96].rearrange("p h (two f) -> p h two f", two=2), [bk(bq), "cosB", "sinB"], [("Q_tok")],
             scale_ap=sm[:, 1:2], scale_key="sm1")
        kv4 = ps[bkv][:, 0:512].rearrange("p (h c) -> p h c", h=4)
        tr.op(ACT, lambda: nc.scalar.activation(out=K_tok[:, :, 0:64], in_=kv4[:, :, 0:64], func=AF.Copy, scale=sm[:, 3:4]), r=[bk(bkv), "sm3"], w=["K_tok"])
        tr.op(ACT, lambda: nc.scalar.activation(out=V_B[:, t, :, 0:64], in_=kv4[:, :, 64:128], func=AF.Copy, scale=sm[:, 3:4]), r=[bk(bkv), "sm3"], w=[("V_B", t)])
        bc = bankA()
        ck2 = cqk_tok[:].rearrange("p h two f -> p (h two f)")
        transposes([(psb[bc][:, j * 128:(j + 1) * 128], ck2[:, j * 128:(j + 1) * 128]) for j in range(3)], bk(bc), ["cqk_tok"])
        tr.op(DVE, lambda: nc.vector.tensor_copy(cqT[0:64, 0:2, :], psb[bc][0:64, 0:256].rearrange("p (a c) -> p a c", a=2)), r=[bk(bc)], w=[("cqT", par)])
        tr.op(DVE, lambda: nc.vector.tensor_copy(cqT[64:128, 2:4, :], psb[bc][64:128, 0:256].rearrange("p (a c) -> p a c", a=2)), r=[bk(bc)], w=[("cqT", par)])
        tr.op(DVE, lambda: nc.vector.tensor_copy(KT_C[:, sC, :], psb[bc][:, 256:384]), r=[bk(bc)], w=[("KT_C", sC)])
        bqt = bankA()
        transposes([(psb[bqt][0:96, h * 128:(h + 1) * 128], Q_tok[:, h, :]) for h in range(4)], bk(bqt), ["Q_tok"])
        tr.op(ACT, lambda: nc.scalar.copy(out=QT[0:96, :, :].rearrange("p h c -> p (h c)"), in_=psb[bqt][0:96, 0:512]), r=[bk(bqt)], w=[("QT", par)])
        bkt = bankA()
        transposes([(psb[bkt][0:96, h * 128:(h + 1) * 128], K_tok[:, h, :]) for h in range(4)], bk(bkt), ["K_tok"])
        tr.op(ACT, lambda: nc.scalar.copy(out=KT_B[0:96, :, t * 128:(t + 1) * 128],
                                          in_=psb[bkt][0:96, 0:512].rearrange("p (h c) -> p h c", h=4)),
              r=[bk(bkt)], w=[("KT_B", t)])


    def stageB(l, t, dst, dkey, par):
        sA = t % 8
        sC = t % 4
        aqT, mqT, cqT, QT, sg = aqT_l[par], mqT_l[par], cqT_l[par], QT_l[par], sg_l[par]
        def oslice(g, h):
            n = g * 4 + h
            return n // 7, (n % 7) * 65

        def zinit():
            ins = None
            for i in range(3):
                ins = nc.tensor.matmul(ps[i][:, :], zeros[:, 0:128], w_out_sb[:, 0, 0:512], start=True, stop=True)
            return ins
        tr.op(PE, zinit, r=["zeros", ("w_out", 0)], w=[bk(0), bk(1), bk(2)])

        blocks = []
        for mb in range(2):
            qk = [(KmT[:, h // 2, mb * 128:(mb + 1) * 128], mqT[:, h, :]) for h in range(4)]
            blocks.append((qk, ["KmT", ("mqT", par)], "full", 0.125, None, [VmA[:, mb, h, :] for h in range(4)], ["VmA"], 3))
        for jb in range(max(0, t - 4), t + 1):
            j = jb - (t - 4)
            sl = jb % 8
            qk = [(KT_A[:, h // 2, sl, :], aqT[:, h, :]) for h in range(4)]
            mode = "lo" if j == 0 else ("hi" if j == 4 else "full")
            ti = 0 if j <= 2 else (1 if j == 3 else 2)
            blocks.append((qk, [("KT_A", sl), ("aqT", par)], mode, 0.125, ti, [V_A[:, sl, h, :] for h in range(4)], [("V_A", sl)], 0))
        for jb in range(max(0, t - 1), t + 1):
            sl = jb % 4
            qk = [(KT_C[:, sl, :], cqT[:, h, :]) for h in range(4)]
            mode = "hi" if jb == t else "lo"
            blocks.append((qk, [("KT_C", sl), ("cqT", par)], mode, 0.125, None, [V_C[:, sl, h // 2, :] for h in range(4)], [("V_C", sl)], 2))
        for jb in range(0, t + 1):
            qk = [(KT_B[0:96, h, jb * 128:(jb + 1) * 128], QT[0:96, h, :]) for h in range(4)]
            mode = "hi" if jb == t else "full"
            blocks.append((qk, [("KT_B", jb), ("QT", par)], mode, SCALE_B, None, [V_B[:, jb, h, :] for h in range(4)], [("V_B", jb)], 1))

        nblk = len(blocks)
        pending = []

        def emit_pv(item):
            pt, ptk, vl, vk, g, last = item

            def fn():
                ins = None
                for h in range(4):
                    bi, off = oslice(g, h)
                    ins = nc.tensor.matmul(ps[bi][:, off:off + 65], pt[:, h * 128:(h + 1) * 128], vl[h], start=False, stop=last,
                                           skip_group_check=True)
                return ins
            okeys = sorted({bk(oslice(g, h)[0]) for h in range(4)})
            tr.op(PE, fn, r=[ptk] + vk, w=okeys)

        for bi_, (qk, qkeys, mode, scale, ti, vl, vk, g) in enumerate(blocks):
            b = bankB()

            def fqk(b=b, qk=qk, ti=ti):
                ins = None
                if ti is None:
                    for h in range(4):
                        ins = nc.tensor.matmul(ps[b][:, h * 128:(h + 1) * 128], qk[h][0], qk[h][1], start=True, stop=True)
                    return ins
                for h in range(4):
                    nc.tensor.matmul(ps[b][:, h * 128:(h + 1) * 128], qk[h][0], qk[h][1], start=(h == 0), stop=False,
                                     skip_group_check=True)
                nc.tensor.matmul(ps[b][:, :], ident[:, :], tblh[:, ti, :], start=False, stop=False, skip_group_check=True)
                return nc.tensor.matmul(ps[b][:, :], ident[:, :], tbll[:, ti, :], start=False, stop=True, skip_group_check=True)
            tr.op(PE, fqk, r=qkeys + ([] if ti is None else [("tblh", ti), ("tbll", ti), "ident"]), w=[bk(b)])
            src_ap, src_key, sc = ps[b], bk(b), scale
            if mode == "full":
                i_ = pt_rr[0] % NPT
                pt_rr[0] += 1
                pt, ptk = PT[i_], ("PT", i_)
                tr.op(ACT, lambda pt=pt, src_ap=src_ap, sc=sc: nc.scalar.activation(out=pt[:, :], in_=src_ap[:, :], func=AF.Exp, scale=sc),
                      r=[src_key], w=[ptk])
            else:
                if mode == "lo":
                    i_ = lo_rr[0] % 1
                    lo_rr[0] += 1
                    pt, ptk = PT_lo[i_], ("PTlo", i_)
                    full_p, part_p, q0 = slice(64, 128), slice(0, 64), 0
                else:
                    i_ = hi_rr[0] % 3
                    hi_rr[0] += 1
                    pt, ptk = PT_hi[i_], ("PThi", i_)
                    full_p, part_p, q0 = slice(0, 64), slice(64, 128), 64

                def fexp(pt=pt, src_ap=src_ap, sc=sc, full_p=full_p, part_p=part_p, q0=q0):
                    nc.scalar.activation(out=pt[full_p, :], in_=src_ap[full_p, :], func=AF.Exp, scale=sc)
                    return nc.scalar.activation(out=pt[part_p, :].rearrange("p (h q) -> p h q", h=4)[:, :, q0:q0 + 64],
                                                in_=src_ap[part_p, :].rearrange("p (h q) -> p h q", h=4)[:, :, q0:q0 + 64],
                                                func=AF.Exp, scale=sc)
                tr.op(ACT, fexp, r=[src_key], w=[ptk])
            last = (bi_ == nblk - 1) or (blocks[bi_ + 1][7] != g)
            pending.append((pt, ptk, vl, vk, g, last))
            if len(pending) > 2:
                emit_pv(pending.pop(0))
        while pending:
            emit_pv(pending.pop(0))

        for i in range(3):
            nsl = 7 if i < 2 else 2
            tr.op(DVE, lambda i=i, nsl=nsl: nc.vector.tensor_copy(den[:, i * 7:i * 7 + nsl],
                                                                  ps[i][:, 0:nsl * 65].rearrange("p (n c) -> p n c", c=65)[:, :, 64]),
                  r=[bk(i)], w=["den"])
        tr.op(DVE, lambda: nc.vector.tensor_tensor(den[:, 8:12], den[:, 8:12], esink[:, :], ALU.add), r=["den", "esink"], w=["den"])
        tr.op(DVE, lambda: nc.vector.tensor_scalar(out=den[:, :], in0=den[:, :], scalar1=2.0, scalar2=None, op0=ALU.mult), r=["den"], w=["den"])
        tr.op(DVE, lambda: nc.vector.reciprocal(rec[:, :], den[:, :]), r=["den"], w=["rec"])

        def fgate():
            ins = None
            for n in range(16):
                bi, off = n // 7, (n % 7) * 65
                ins = nc.vector.scalar_tensor_tensor(out=att[:, n * 64:(n + 1) * 64], in0=ps[bi][:, off:off + 64], scalar=rec[:, n:n + 1],
                                                     in1=sg[:, n * 64:(n + 1) * 64], op0=ALU.mult, op1=ALU.mult)
            return ins
        tr.op(DVE, fgate, r=[bk(0), bk(1), bk(2), "rec", ("sg", par, 0), ("sg", par, 1)], w=["att"])
    def stageB2(l, t, dst, dkey, par):
        bg_ = bankC()
        transposes([(psb[bg_][:, k * 128:(k + 1) * 128], att[:, k * 128:(k + 1) * 128]) for k in range(8)], bk(bg_), ["att"])
        tr.op(ACT, lambda: nc.scalar.copy(out=gT[:].rearrange("p k c -> p (k c)"), in_=psb[bg_][:, 0:1024]), r=[bk(bg_)], w=["gT"])
        zt = z[par]
        for n in range(2):
            by = bankC()
            mm_group(ps[by][:, :], [(gT[:, k, :], w_out_sb[:, k, n * 512:(n + 1) * 512]) for k in range(8)], bk(by), ["gT"] + [("w_out", k) for k in range(8)])
            tr.op(DVE, lambda by=by, n=n: nc.vector.scalar_tensor_tensor(out=zt[:, n * 512:(n + 1) * 512], in0=zt[:, n * 512:(n + 1) * 512],
                                                                        scalar=ALPHA, in1=ps[by][:, :], op0=ALU.mult, op1=ALU.add),
                  r=[bk(by), ("z", par)], w=[("z", par)])
        def fstats():
            nc.vector.bn_stats(st6[:, 2, :], zt[:, 0:512])
            return nc.vector.bn_stats(st6[:, 3, :], zt[:, 512:1024])
        tr.op(DVE, fstats, r=[("z", par)], w=["st6c"])
        tr.op(DVE, lambda: nc.vector.bn_aggr(mv[:, 2, :], st6[:, 2:4, :].rearrange("p a b -> p (a b)")), r=["st6c"], w=["mv2"])
        rsqrt_chain("mv2", mv[:, 2, 1:2], 1e-5, sm[:, 4:5], "sm4")
        tr.op(DVE, lambda: nc.vector.scalar_tensor_tensor(out=sm[:, 5:6], in0=mv[:, 2, 0:1], scalar=-1.0, in1=sm[:, 4:5],
                                                         op0=ALU.mult, op1=ALU.mult), r=["mv2", "sm4"], w=["sm5"])
        tr.op(POOL, lambda: nc.gpsimd.tensor_scalar(out=zt[:, :], in0=zt[:, :], scalar1=sm[:, 4:5], scalar2=sm[:, 5:6],
                                                    op0=ALU.mult, op1=ALU.add),
              r=[("z", par), "sm4", "sm5"], w=[("z", par)])
        tr.op(POOL, lambda: nc.gpsimd.tensor_tensor(zt[:, :], zt[:, :], G_sb[:, :], ALU.mult), r=[("z", par), "G"], w=[("z", par)])
        tr.op(POOL, lambda: nc.gpsimd.tensor_tensor(zt[:, :], zt[:, :], B_sb[:, :], ALU.add), r=[("z", par), "Bv"], w=[("z", par)])
        tr.op(SP, lambda: nc.sync.dma_start(out=dst[t * 128:(t + 1) * 128, :], in_=zt[:, :]),
              r=[("z", par)], w=[(dkey, t)], dma=True)

    AFRAC = 0.75
    BOFF = 0.15
    import os as _os
    MERGE_OFF = bool(_os.environ.get('KMERGE_OFF'))

    def merge(LA, LB):
        if MERGE_OFF:
            return list(LA) + list(LB)
        na, nb = max(len(LA), 1), max(len(LB), 1)
        items = [(AFRAC * (i + 0.5) / na, 0, i, a) for i, a in enumerate(LA)]
        items += [((i + 0.5) / nb, 1, i, b) for i, b in enumerate(LB)]
        items.sort(key=lambda x_: (x_[0], x_[1], x_[2]))
        return [x_[3] for x_ in items]

    def load_xb(src, skey, t, slot):
        tr.op(POOL, lambda: nc.gpsimd.dma_start(out=xb[slot][:], in_=src[t * 128:(t + 1) * 128, :]),
              r=[(skey, t)], w=[("xb", slot)], dma=True)

    def load_z(src, skey, t, par):
        tr.op(SP, lambda: nc.sync.dma_start(out=z[par][:], in_=src[t * 128:(t + 1) * 128, :]),
              r=[(skey, t)], w=[("z", par)], dma=True)

    def merge3(LA, LB, LC):
        if MERGE_OFF:
            return list(LC) + list(LA) + list(LB)
        items = []
        for off, frac, pri, L in ((0.0, AFRAC, 1, LA), (BOFF, 1.0 - BOFF, 2, LB), (0.30, 0.45, 0, LC)):
            n = max(len(L), 1)
            items += [(off + frac * (i + 0.5) / n, pri, i, a) for i, a in enumerate(L)]
        items.sort(key=lambda x_: (x_[0], x_[1], x_[2]))
        return [x_[3] for x_ in items]

    sl_list = [(s_, l_) for s_ in range(nseq) for l_ in range(nl)]

    def io(s_, l_):
        src_ = x_d[s_] if l_ == 0 else hb_d[(l_ - 1) % 2]
        dst_ = out_d[s_] if l_ == nl - 1 else hb_d[l_ % 2]
        skey_ = ("xin", s_) if l_ == 0 else ("hb", (l_ - 1) % 2)
        dkey_ = ("xout", s_) if l_ == nl - 1 else ("hb", l_ % 2)
        return src_, dst_, skey_, dkey_

    early_w = set()
    early_x = set()

    def early(idx, gt0):
        s_, l_ = sl_list[idx]
        src_, _, skey_, _ = io(s_, l_)
        if l_ != 0 and ntiles > 0:
            load_xb(src_, skey_, 0, gt0 % 2)
            if ntiles > 1:
                load_xb(src_, skey_, 1, (gt0 + 1) % 2)
            early_x.add(idx)
        layer_setup_early(l_)
        early_w.add(idx)

    gt = 0
    for idx, (s, l) in enumerate(sl_list):
        if l == 0:
            seq_setup(s)
        src, dst, skey, dkey = io(s, l)
        if idx not in early_w:
            layer_setup_early(l)
        layer_setup_late(l)
        mem_kv()
        if idx not in early_x and ntiles > 0:
            load_xb(src, skey, 0, gt % 2)
            if ntiles > 1:
                load_xb(src, skey, 1, (gt + 1) % 2)
        if ntiles > 0:
            load_z(src, skey, 0, gt % 2)
            xT_step(gt % 2)
            stageA(l, 0, gt % 2, gt % 2)
            if ntiles > 1:
                xT_step((gt + 1) % 2)
        for t in range(ntiles):
            LA, LC = [], []
            if t + 1 < ntiles:
                tr.begin()
                if t + 2 < ntiles:
                    load_xb(src, skey, t + 2, gt % 2)
                stageA(l, t + 1, (gt + 1) % 2, (gt + 1) % 2)
                if t + 2 < ntiles:
                    xT_step(gt % 2)
                LA = tr.end()
            if t > 0:
                tr.begin()
                stageB2(l, t - 1, dst, dkey, (gt - 1) % 2)
                LC = tr.end()
            tr.begin()
            stageB(l, t, dst, dkey, gt % 2)
            LB = tr.end()
            tr.commit(merge3(LA, LB, LC))
            if t + 1 < ntiles:
                load_z(src, skey, t + 1, (gt + 1) % 2)
            gt += 1
            if t == ntiles - 2 and idx + 1 < len(sl_list):
                early(idx + 1, gt + 1)
        if ntiles > 0:
            stageB2(l, ntiles - 1, dst, dkey, (gt - 1) % 2)
    tr.emit()
    return nc, es


def _prep_shared(w_in, rel_bias, mla_q_norm, w_uq, mla_kv_norm, w_ukv, swa_sinks, w_mem_kv, w_out, ln_gain, ln_bias):
    r = lambda a, b: np.arange(a, b)
    cq_perm = np.concatenate([1632 + np.arange(64) + 64 * h for h in (0, 2, 1, 3)])
    cols = np.concatenate([
        r(0, 256), r(256, 512), r(2400, 2656),
        r(768, 1024), r(1376, 1632),
        r(2144, 2400), r(2656, 2912),
        cq_perm, r(1888, 2016), r(2016, 2144),
        r(512, 768), r(1024, 1216), r(1344, 1376),
        r(1216, 1344),
    ])
    assert cols.shape[0] == PW and np.unique(cols).shape[0] == PW
    w_in_p = np.ascontiguousarray(np.take(np.asarray(w_in, np.float32), cols, axis=2))
    rb = np.asarray(rel_bias, np.float32)
    i = np.arange(128)[:, None]
    q = np.arange(128)[None, :]
    idx3 = np.minimum(256 + q - i, 256)
    idx4 = 128 + q - i
    idx0 = np.full((128, 128), 256)
    idx = np.stack([idx0, idx3, idx4], axis=1)
    tbl = rb[:, :, idx]
    tbl = np.ascontiguousarray(np.transpose(tbl, (0, 2, 3, 1, 4))).reshape(NL, 128, 3, 512)
    ident = np.eye(128, dtype=np.float32)
    invfC = (10000.0 ** (-np.arange(0, 64, 2, dtype=np.float32) / 64)).astype(np.float32)
    invfB = (10000.0 ** (-np.arange(0, 32, 2, dtype=np.float32) / 32)).astype(np.float32)
    invf = np.ascontiguousarray(np.broadcast_to(np.concatenate([invfC, invfB])[None, :], (128, 48))).astype(np.float32)
    f = lambda a: np.ascontiguousarray(np.asarray(a, np.float32))
    return {"w_in": w_in_p, "relb": tbl, "qn": f(mla_q_norm), "w_uq": f(w_uq), "kvn": f(mla_kv_norm), "w_ukv": f(w_ukv),
            "sinks": f(swa_sinks), "w_mem_kv": f(w_mem_kv), "w_out": f(w_out), "ln_gain": f(ln_gain), "ln_bias": f(ln_bias),
            "ident": ident, "invf": invf}


_CACHE = {}


def kernel(x, mem, positions, w_in, rel_bias, mla_q_norm, w_uq, mla_kv_norm, w_ukv, swa_sinks, w_mem_kv, w_out, ln_gain, ln_bias):
    shared = _prep_shared(w_in, rel_bias, mla_q_norm, w_uq, mla_kv_norm, w_ukv, swa_sinks, w_mem_kv, w_out, ln_gain, ln_bias)
    x = np.asarray(x, np.float32)
    mem = np.asarray(mem, np.float32)
    positions = np.asarray(positions, np.int32)
    in_maps = []
    for c in range(NCORES):
        sl = slice(c * NSEQ, (c + 1) * NSEQ)
        pos = np.ascontiguousarray(np.transpose(positions[sl].reshape(NSEQ, T, 128), (0, 2, 1)))
        m = {"x": np.ascontiguousarray(x[sl]), "mem": np.ascontiguousarray(mem[sl]), "pos": pos}
        m.update(shared)
        in_maps.append(m)
    if "nc" not in _CACHE:
        _CACHE["nc"] = build_program()
    nc, _es = _CACHE["nc"]
    res = run_bass_kernel_spmd(nc, in_maps, core_ids=list(range(NCORES)))
    out = np.concatenate([np.asarray(r["out"], np.float32) for r in res.results], axis=0)
    return out
```

```python
import math
from contextlib import ExitStack
import numpy as np
import concourse.bass as bass
import concourse.mybir as mybir
from concourse.bass_utils import run_bass_kernel_spmd

F32 = mybir.dt.float32
BF16 = mybir.dt.bfloat16
I32 = mybir.dt.int32
ALU = mybir.AluOpType
AF = mybir.ActivationFunctionType

NL = 4
NSEQ = 2
S = 4096
D = 1024
T = S // 128
NCORES = 8
PW = 2912
ALPHA = (2.0 * NL) ** 0.25
TWO_PI = 2.0 * math.pi
C1 = float(np.float32(6.28125))
C2 = float(np.float32(TWO_PI - 6.28125))
PI_LO = float(np.nextafter(np.float32(math.pi), np.float32(0)))
SCALE_B = 96.0 ** -0.5

FM0 = 0
G0 = 768
G1 = 1280
T2 = 1792
T3 = 2304
T4 = 2784


class _Op:
    __slots__ = ("eng", "fn", "deps", "dma", "eidx", "sem", "val", "need")

    def __init__(self, eng, fn, deps, dma):
        self.eng, self.fn, self.deps, self.dma = eng, fn, deps, dma
        self.eidx = 0
        self.sem = None
        self.val = 0
        self.need = False


class Tracker:
    EPOCH = 30000
    NDMA = 20
    NEAR = 1 << 30
    LIMIT = None

    def __init__(self, nc, es):
        self.nc = nc
        self.es = es
        self.engs = {"pe": nc.tensor, "act": nc.scalar, "dve": nc.vector, "pool": nc.gpsimd, "sp": nc.sync}
        self.ops = []
        self.last_w = {}
        self.readers = {}
        self.xlast = {}
        self.rec = None

    def begin(self):
        self.rec = []

    def end(self):
        lst, self.rec = self.rec, None
        return lst

    def commit(self, lst):
        for a in lst:
            self._op(*a)

    def op(self, eng, fn, r=(), w=(), dma=False):
        if self.rec is not None:
            self.rec.append((eng, fn, tuple(r), tuple(w), dma))
            return
        self._op(eng, fn, r, w, dma)

    def _op(self, eng, fn, r=(), w=(), dma=False):
        i = len(self.ops)
        deps = set()
        for k in r:
            j = self.last_w.get(k)
            if j is not None:
                deps.add(j)
            if isinstance(k, tuple) and k[0] == "ps":
                xl = self.xlast.setdefault(k, {})
                for e2, j2 in xl.items():
                    if e2 != eng:
                        deps.add(j2)
                xl[eng] = i
        for k in w:
            j = self.last_w.get(k)
            if j is not None:
                deps.add(j)
            for j in self.readers.get(k, ()):
                deps.add(j)
        for k in r:
            self.readers.setdefault(k, []).append(i)
        for k in w:
            self.last_w[k] = i
            self.readers[k] = []
            self.xlast[k] = {}
        deps.discard(i)
        self.ops.append(_Op(eng, fn, deps, dma))

    def emit(self):
        nc, ops = self.nc, self.ops
        if self.LIMIT is not None:
            ops = self.ops = self.ops[:self.LIMIT]
            for o in ops:
                o.deps = {j for j in o.deps if j < len(ops)}
        cnt = {e: 0 for e in self.engs}
        for o in ops:
            o.eidx = cnt[o.eng]
            cnt[o.eng] += 1
        for o in ops:
            for j in o.deps:
                d = ops[j]
                if d.dma:
                    continue
                if d.eng != o.eng or o.dma:
                    d.need = True
                elif o.eng != "pe" and o.eidx - d.eidx <= self.NEAR:
                    d.need = True
        sem_ctr = [0]

        def new_sem():
            sem_ctr[0] += 1
            return self.es.enter_context(nc.semaphore(f"sm{sem_ctr[0]}"))

        cur_sem = {e: new_sem() for e in self.engs}
        cur_cnt = {e: 0 for e in self.engs}
        dma_sems = [new_sem() for _ in range(self.NDMA)]
        dma_tot = [0] * self.NDMA
        half = self.NDMA // 2
        dma_rr = {"sp": 0, "pool": 0, "act": 0}
        dma_base = {"sp": 0, "pool": half, "act": 0}
        observed = {e: {} for e in self.engs}

        def wait(e, sem, val):
            ob = observed[e]
            k = id(sem)
            if ob.get(k, 0) >= val:
                return
            self.engs[e].wait_ge(sem, val)
            ob[k] = val

        for o in ops:
            e = o.eng
            for j in sorted(o.deps):
                d = ops[j]
                if d.dma or d.need:
                    if (not d.dma) and d.eng == e and not o.dma and not (e != "pe" and o.eidx - d.eidx <= self.NEAR):
                        continue
                    wait(e, d.sem, d.val)
            if o.dma:
                k = dma_base[e] + dma_rr[e]
                dma_rr[e] = (dma_rr[e] + 1) % half
                if dma_tot[k] > 0:
                    wait(e, dma_sems[k], dma_tot[k])
                ins = o.fn()
                dma_tot[k] += 16
                ins.then_inc(dma_sems[k], 16)
                o.sem, o.val = dma_sems[k], dma_tot[k]
            else:
                ins = o.fn()
                if o.need:
                    if cur_cnt[e] >= self.EPOCH:
                        cur_sem[e] = new_sem()
                        cur_cnt[e] = 0
                    cur_cnt[e] += 1
                    ins.then_inc(cur_sem[e], 1)
                    o.sem, o.val = cur_sem[e], cur_cnt[e]
        for k in range(self.NDMA):
            if dma_tot[k] > 0:
                wait("sp", dma_sems[k], dma_tot[k])


def build_program(nl=NL, nseq=NSEQ, ntiles=T):
    nc = bass.Bass("TRN2", target_bir_lowering=False)
    es = ExitStack()
    dt = nc.dram_tensor
    x_d = dt("x", [NSEQ, S, D], F32, kind="ExternalInput").ap()
    mem_d = dt("mem", [NSEQ, 256, D], F32, kind="ExternalInput").ap()
    pos_d = dt("pos", [NSEQ, 128, T], I32, kind="ExternalInput").ap()
    win_d = dt("w_in", [NL, D, PW], F32, kind="ExternalInput").ap()
    tbl_d = dt("relb", [NL, 128, 3, 512], F32, kind="ExternalInput").ap()
    qn_d = dt("qn", [NL, 192], F32, kind="ExternalInput").ap()
    wuq_d = dt("w_uq", [NL, 192, 384], F32, kind="ExternalInput").ap()
    kvn_d = dt("kvn", [NL, 128], F32, kind="ExternalInput").ap()
    wukv_d = dt("w_ukv", [NL, 128, 512], F32, kind="ExternalInput").ap()
    sink_d = dt("sinks", [NL, 4], F32, kind="ExternalInput").ap()
    wmk_d = dt("w_mem_kv", [NL, D, 512], F32, kind="ExternalInput").ap()
    wout_d = dt("w_out", [NL, D, D], F32, kind="ExternalInput").ap()
    lng_d = dt("ln_gain", [NL, D], F32, kind="ExternalInput").ap()
    lnb_d = dt("ln_bias", [NL, D], F32, kind="ExternalInput").ap()
    idn_d = dt("ident", [128, 128], F32, kind="ExternalInput").ap()
    ifc_d = dt("invf", [128, 48], F32, kind="ExternalInput").ap()
    out_d = dt("out", [NSEQ, S, D], F32, kind="ExternalOutput").ap()
    hb_d = dt("hbuf", [2, S, D], F32, kind="Internal").ap()

    def sb(name, shape, dtype):
        return es.enter_context(nc.sbuf_tensor(name, shape, dtype))

    w_in_sb = sb("w_in_sb", [128, 8, PW], BF16)
    w_out_sb = sb("w_out_sb", [128, 8, D], BF16)
    w_uq_sb = sb("w_uq_sb", [128, 2, 384], BF16)
    w_ukv_sb = sb("w_ukv_sb", [128, 512], BF16)
    tblh = sb("tblh", [128, 3, 512], BF16)
    tbll = sb("tbll", [128, 3, 512], BF16)
    G_sb = sb("G_sb", [128, D], F32)
    B_sb = sb("B_sb", [128, D], F32)
    gq_sb = sb("gq_sb", [128, 2], F32)
    gkv_sb = sb("gkv_sb", [128, 1], F32)
    esink = sb("esink", [128, 4], F32)
    KT_B = sb("KT_B", [128, 4, S], BF16)
    V_B = sb("V_B", [128, T, 4, 65], BF16)
    KT_A = sb("KT_A", [128, 2, 8, 128], BF16)
    V_A = sb("V_A", [128, 8, 4, 65], BF16)
    KT_C = sb("KT_C", [128, 4, 128], BF16)
    V_C = sb("V_C", [128, 4, 2, 65], BF16)
    memT = sb("memT", [128, 8, 256], BF16)
    KmT = sb("KmT", [128, 2, 256], BF16)
    VmA = sb("VmA", [128, 2, 4, 65], BF16)
    cosC = sb("cosC", [128, T, 32], F32)
    sinC = sb("sinC", [128, T, 32], F32)
    cosB = sb("cosB", [128, T, 16], F32)
    sinB = sb("sinB", [128, T, 16], F32)
    ident = sb("ident_sb", [128, 128], BF16)
    zeros = sb("zeros", [128, 128], BF16)
    mhalf = sb("mhalf", [128, 1], F32)
    invf = sb("invf_sb", [128, 48], F32)
    posi = sb("posi", [128, T], I32)
    posf = sb("posf", [128, T], F32)
    xb = [sb(f"xb{i}", [128, D], BF16) for i in range(2)]
    xT = sb("xT", [128, 8, 128], BF16)
    zbuf = sb("zbuf", [128, 2, D], F32)
    z = [zbuf[:, i, :] for i in range(2)]
    wmk_sb = zbuf[:].rearrange("p a d -> p (a d)").bitcast(BF16).rearrange("p (k c) -> p k c", k=8)
    sg_l = [sb(f"sg{i}", [128, D], F32) for i in range(2)]
    aqT_l = [sb(f"aqT{i}", [128, 4, 128], BF16) for i in range(2)]
    mqT_l = [sb(f"mqT{i}", [128, 4, 128], BF16) for i in range(2)]
    cqT_l = [sb(f"cqT{i}", [128, 4, 128], BF16) for i in range(2)]
    QT_l = [sb(f"QT{i}", [128, 4, 128], BF16) for i in range(2)]
    sg = sg_l[0]
    Q_tok = sb("Q_tok", [128, 4, 96], BF16)
    K_tok = sb("K_tok", [128, 4, 96], BF16)
    cqk_tok = sb("cqk_tok", [128, 6, 2, 32], BF16)
    rt = [sb(f"rt{i}", [128, 384], F32) for i in range(2)]
    kr = sb("kr", [128, 32], F32)
    cqn = sb("cqn", [128, 192], BF16)
    ckvn = sb("ckvn", [128, 128], BF16)
    cqnT = sb("cqnT", [128, 2, 128], BF16)
    ckvnT = sb("ckvnT", [128, 128], BF16)
    st6 = sb("st6", [128, 4, 6], F32)
    mv = sb("mv", [128, 3, 2], F32)
    sm = sb("sm", [128, 8], F32)
    NPT = 3
    PT = [sb(f"PT{i}", [128, 512], BF16) for i in range(NPT)]
    PT_lo = [sb(f"PTlo{i}", [128, 512], BF16) for i in range(1)]
    PT_hi = [sb(f"PThi{i}", [128, 512], BF16) for i in range(3)]
    den = sb("den", [128, 16], F32)
    rec = sb("rec", [128, 16], F32)
    att = sb("att", [128, D], BF16)
    gT = sb("gT", [128, 8, 128], BF16)

    ps = [es.enter_context(nc.psum_tensor(f"ps{i}", [128, 512], F32)) for i in range(8)]
    psb = [p[:].bitcast(BF16) for p in ps]

    tr = Tracker(nc, es)
    WINK = [("w_in", k, c0) for k in range(8) for c0 in (0, 1456)]
    PE, ACT, DVE, POOL, SP = "pe", "act", "dve", "pool", "sp"
    rotA = [0]
    rotB = [0]

    def bankA():
        i = 6 + rotA[0] % 2
        rotA[0] += 1
        return i

    def bankB():
        i = 3 + rotB[0] % 2
        rotB[0] += 1
        return i

    def bankC():
        return 5

    bank = bankB

    def bk(i):
        return ("ps", i)

    tr.op(POOL, lambda: nc.gpsimd.dma_start(out=ident[:], in_=idn_d[:, :]), w=["ident"], dma=True)
    tr.op(SP, lambda: nc.sync.dma_start(out=invf[:], in_=ifc_d[:, :]), w=["invf"], dma=True)
    tr.op(POOL, lambda: nc.gpsimd.memset(zeros[:], 0.0), w=["zeros"])
    tr.op(POOL, lambda: nc.gpsimd.memset(mhalf[:], -0.5), w=["mhalf"])
    for i in range(1):
        tr.op(POOL, lambda i=i: nc.gpsimd.memset(PT_lo[i][:], 0.0), w=[("PTlo", i)])
    for i in range(3):
        tr.op(POOL, lambda i=i: nc.gpsimd.memset(PT_hi[i][:], 0.0), w=[("PThi", i)])
    tr.op(POOL, lambda: nc.gpsimd.memset(V_A[:], 1.0), w=[("V_A", i) for i in range(8)])
    tr.op(POOL, lambda: nc.gpsimd.memset(V_C[:], 1.0), w=[("V_C", i) for i in range(4)])
    tr.op(POOL, lambda: nc.gpsimd.memset(V_B[:], 1.0), w=[("V_B", i) for i in range(T)])
    tr.op(POOL, lambda: nc.gpsimd.memset(VmA[:], 1.0), w=["VmA"])
    tr.op(POOL, lambda: nc.gpsimd.memset(cqnT[:], 0.0), w=["cqnT"])
    for i in range(2):
        tr.op(POOL, lambda i=i: nc.gpsimd.memset(aqT_l[i][:], 0.0), w=[("aqT", i)])
        tr.op(POOL, lambda i=i: nc.gpsimd.memset(mqT_l[i][:], 0.0), w=[("mqT", i)])
        tr.op(POOL, lambda i=i: nc.gpsimd.memset(cqT_l[i][:], 0.0), w=[("cqT", i)])
    tr.op(POOL, lambda: nc.gpsimd.memset(w_uq_sb[:], 0.0), w=["w_uq"])

    SGK = [("sg", 0, 0), ("sg", 0, 1)]
    lo_rr = [0]
    hi_rr = [0]
    pt_rr = [0]
    tmp_rr = [0]

    def rsqrt_chain(src_key, v_ap, eps, out_ap, out_key):
        tr.op(DVE, lambda: nc.vector.tensor_scalar(out=v_ap, in0=v_ap, scalar1=eps, scalar2=None, op0=ALU.add),
              r=[src_key], w=[src_key])
        tr.op(POOL, lambda: nc.gpsimd.tensor_tensor(out_ap, v_ap, mhalf[:], ALU.pow),
              r=[src_key, "mhalf"], w=[out_key])

    def rope(xv, nh, f, cos_ap, sin_ap, ov, src_keys, dst_keys, scale_ap=None, scale_key=None):
        n = nh * 2 * f
        t1 = rt[0][:, 0:n].rearrange("p (h two f) -> p h two f", h=nh, two=2)
        t2 = rt[1][:, 0:n].rearrange("p (h two f) -> p h two f", h=nh, two=2)
        cb = cos_ap.unsqueeze(1).unsqueeze(1).broadcast_to([128, nh, 2, f])
        sbb = sin_ap.unsqueeze(1).broadcast_to([128, nh, f])
        sk = list(src_keys) + ([scale_key] if scale_key is not None else [])

        def mul(o_, a_, b_):
            if scale_ap is None:
                return lambda: nc.vector.tensor_tensor(o_, a_, b_, ALU.mult)
            return lambda: nc.vector.scalar_tensor_tensor(out=o_, in0=a_, scalar=scale_ap, in1=b_, op0=ALU.mult, op1=ALU.mult)
        if scale_ap is None:
            tr.op(DVE, mul(t1, xv, cb), r=sk, w=[("rt", 0)])
        else:
            cb3 = cos_ap.unsqueeze(1).broadcast_to([128, nh, f])
            tr.op(DVE, mul(t1[:, :, 0, :], xv[:, :, 0, :], cb3), r=sk, w=[("rt", 0)])
            tr.op(DVE, mul(t1[:, :, 1, :], xv[:, :, 1, :], cb3), r=sk, w=[("rt", 0)])
        tr.op(DVE, mul(t2[:, :, 0, :], xv[:, :, 1, :], sbb), r=sk, w=[("rt", 1)])
        tr.op(DVE, mul(t2[:, :, 1, :], xv[:, :, 0, :], sbb), r=sk, w=[("rt", 1)])
        tr.op(DVE, lambda: nc.vector.tensor_tensor(ov[:, :, 0, :], t1[:, :, 0, :], t2[:, :, 0, :], ALU.subtract),
              r=[("rt", 0), ("rt", 1)], w=dst_keys)
        tr.op(DVE, lambda: nc.vector.tensor_tensor(ov[:, :, 1, :], t1[:, :, 1, :], t2[:, :, 1, :], ALU.add),
              r=[("rt", 0), ("rt", 1)], w=dst_keys)

    def mm_group(out_ap, pairs, okey, rkeys):
        def fn():
            n = len(pairs)
            ins = None
            for i, (l, r_) in enumerate(pairs):
                ins = nc.tensor.matmul(out_ap, l, r_, start=(i == 0), stop=(i == n - 1))
            return ins
        tr.op(PE, fn, r=rkeys, w=[okey])

    def transposes(items, okey, rkeys):
        def fn():
            ins = None
            for (o_, i_) in items:
                ins = nc.tensor.transpose(o_, i_, ident[:])
            return ins
        tr.op(PE, fn, r=rkeys + ["ident"], w=[okey])

    def seq_setup(s):
        tr.op(SP, lambda: nc.sync.dma_start(out=posi[:], in_=pos_d[s, :, :]), w=["posi"], dma=True)
        tr.op(DVE, lambda: nc.vector.tensor_copy(posf[:], posi[:]), r=["posi"], w=["posf"])
        for (cos_t, sin_t, f0, nf, ck, sk) in ((cosC, sinC, 0, 32, "cosC", "sinC"), (cosB, sinB, 32, 16, "cosB", "sinB")):
            n = T * nf
            ang = sg[:, 0:n]
            kf = z[0][:, 0:n]
            ki = z[1][:, 0:n].bitcast(I32)
            r2 = z[1][:, 0:n]
            ang3 = ang.rearrange("p (t f) -> p t f", f=nf)

            def f_ang(ang3=ang3, f0=f0, nf=nf):
                ins = None
                for t in range(T):
                    ins = nc.vector.tensor_scalar(out=ang3[:, t, :], in0=invf[:, f0:f0 + nf], scalar1=posf[:, t:t + 1],
                                                  scalar2=None, op0=ALU.mult)
                return ins
            tr.op(DVE, f_ang, r=["posf", "invf"], w=SGK)
            tr.op(DVE, lambda kf=kf, ang=ang: nc.vector.tensor_scalar(out=kf, in0=ang, scalar1=1.0 / TWO_PI, scalar2=None, op0=ALU.mult),
                  r=SGK, w=[("z", 0)])
            tr.op(DVE, lambda ki=ki, kf=kf: nc.vector.tensor_copy(ki, kf), r=[("z", 0)], w=[("z", 1)])
            tr.op(DVE, lambda ki=ki, kf=kf: nc.vector.tensor_copy(kf, ki), r=[("z", 1)], w=[("z", 0)])
            tr.op(DVE, lambda r2=r2, kf=kf, ang=ang: nc.vector.scalar_tensor_tensor(out=r2, in0=kf, scalar=-C1, in1=ang, op0=ALU.mult, op1=ALU.add),
                  r=[("z", 0)] + SGK, w=[("z", 1)])
            tr.op(DVE, lambda r2=r2, kf=kf: nc.vector.scalar_tensor_tensor(out=r2, in0=kf, scalar=-C2, in1=r2, op0=ALU.mult, op1=ALU.add),
                  r=[("z", 0), ("z", 1)], w=[("z", 1)])

            def wrap(buf, key):
                key = key if isinstance(key, list) else [key]
                m = z[0][:, 0:n]
                tr.op(DVE, lambda: nc.vector.tensor_single_scalar(out=m, in_=buf, scalar=math.pi, op=ALU.is_gt), r=key, w=[("z", 0)])
                tr.op(DVE, lambda: nc.vector.scalar_tensor_tensor(out=buf, in0=m, scalar=-TWO_PI, in1=buf, op0=ALU.mult, op1=ALU.add),
                      r=[("z", 0)] + key, w=key)
                tr.op(DVE, lambda: nc.vector.tensor_single_scalar(out=m, in_=buf, scalar=-math.pi, op=ALU.is_lt), r=key, w=[("z", 0)])
                tr.op(DVE, lambda: nc.vector.scalar_tensor_tensor(out=buf, in0=m, scalar=TWO_PI, in1=buf, op0=ALU.mult, op1=ALU.add),
                      r=[("z", 0)] + key, w=key)
                tr.op(DVE, lambda: nc.vector.tensor_scalar(out=buf, in0=buf, scalar1=PI_LO, scalar2=-PI_LO, op0=ALU.min, op1=ALU.max),
                      r=key, w=key)
            wrap(r2, ("z", 1))
            tr.op(ACT, lambda sin_t=sin_t, r2=r2: nc.scalar.activation(out=sin_t[:].rearrange("p t f -> p (t f)"), in_=r2, func=AF.Sin),
                  r=[("z", 1)], w=[sk])
            c2 = sg[:, 0:n]
            tr.op(DVE, lambda c2=c2, r2=r2: nc.vector.tensor_scalar(out=c2, in0=r2, scalar1=math.pi / 2, scalar2=None, op0=ALU.add),
                  r=[("z", 1)], w=SGK)
            wrap(c2, SGK)
            tr.op(ACT, lambda cos_t=cos_t, c2=c2: nc.scalar.activation(out=cos_t[:].rearrange("p t f -> p (t f)"), in_=c2, func=AF.Sin),
                  r=SGK, w=[ck])
        for mb in range(2):
            tr.op(POOL, lambda mb=mb: nc.gpsimd.dma_start(out=xb[mb][:], in_=mem_d[s, mb * 128:(mb + 1) * 128, :]),
                  w=[("xb", mb)], dma=True)
            b = bank()
            transposes([(psb[b][:, k * 128:(k + 1) * 128], xb[mb][:, k * 128:(k + 1) * 128]) for k in range(8)],
                       bk(b), [("xb", mb)])
            tr.op(ACT, lambda b=b, mb=mb: nc.scalar.copy(out=memT[:, :, mb * 128:(mb + 1) * 128],
                                                        in_=psb[b][:, 0:1024].rearrange("p (k c) -> p k c", k=8)),
                  r=[bk(b)], w=["memT"])

    def layer_setup_early(l):
        for k in range(8):
            for (c0, c1) in ((0, 1456), (1456, PW)):
                tr.op(POOL, lambda k=k, c0=c0, c1=c1: nc.gpsimd.dma_start(out=w_in_sb[:, k, c0:c1], in_=win_d[l, k * 128:(k + 1) * 128, c0:c1]),
                      w=[("w_in", k, c0)], dma=True)
        tr.op(POOL, lambda: nc.gpsimd.dma_start(out=w_uq_sb[:, 0, :], in_=wuq_d[l, 0:128, :]), w=["w_uq"], dma=True)
        tr.op(POOL, lambda: nc.gpsimd.dma_start(out=w_uq_sb[0:64, 1, :], in_=wuq_d[l, 128:192, :]), w=["w_uq"], dma=True)
        tr.op(POOL, lambda: nc.gpsimd.dma_start(out=w_ukv_sb[:, :], in_=wukv_d[l, :, :]), w=["w_ukv"], dma=True)
        tr.op(SP, lambda: nc.sync.dma_start(out=gq_sb[:, 0:1], in_=qn_d[l, 0:128].rearrange("(p o) -> p o", o=1)), w=["gq"], dma=True)
        tr.op(SP, lambda: nc.sync.dma_start(out=gq_sb[0:64, 1:2], in_=qn_d[l, 128:192].rearrange("(p o) -> p o", o=1)), w=["gq"], dma=True)
        tr.op(SP, lambda: nc.sync.dma_start(out=gkv_sb[:, 0:1], in_=kvn_d[l, :].rearrange("(p o) -> p o", o=1)), w=["gkv"], dma=True)

    def layer_setup_late(l):
        for k in range(8):
            tr.op(POOL, lambda k=k: nc.gpsimd.dma_start(out=wmk_sb[:, k, :], in_=wmk_d[l, k * 128:(k + 1) * 128, :]),
                  w=[("wmk", k), ("z", 0), ("z", 1)], dma=True)
        for k in range(8):
            tr.op(POOL, lambda k=k: nc.gpsimd.dma_start(out=w_out_sb[:, k, :], in_=wout_d[l, k * 128:(k + 1) * 128, :]),
                  w=[("w_out", k)], dma=True)
        for i in range(3):
            stg = sg_l[i // 2][:, (i % 2) * 512:(i % 2 + 1) * 512]
            sk = ("sg", i // 2, i % 2)
            tr.op(SP, lambda i=i, stg=stg: nc.sync.dma_start(out=stg, in_=tbl_d[l, :, i, :]), w=[sk], dma=True)
            tr.op(POOL, lambda stg=stg: nc.gpsimd.tensor_scalar(out=stg, in0=stg, scalar1=8.0, scalar2=None, op0=ALU.mult), r=[sk], w=[sk])
            tr.op(POOL, lambda i=i, stg=stg: nc.gpsimd.tensor_copy(tblh[:, i, :], stg), r=[sk], w=[("tblh", i)])
            tr.op(POOL, lambda i=i, stg=stg: nc.gpsimd.tensor_tensor(stg, stg, tblh[:, i, :], ALU.subtract), r=[sk, ("tblh", i)], w=[sk])
            tr.op(POOL, lambda i=i, stg=stg: nc.gpsimd.tensor_copy(tbll[:, i, :], stg), r=[sk], w=[("tbll", i)])
        tr.op(SP, lambda: nc.sync.dma_start(out=G_sb[:], in_=lng_d[l, :].partition_broadcast(128)), w=["G"], dma=True)
        tr.op(SP, lambda: nc.sync.dma_start(out=B_sb[:], in_=lnb_d[l, :].partition_broadcast(128)), w=["Bv"], dma=True)
        tr.op(SP, lambda: nc.sync.dma_start(out=esink[:], in_=sink_d[l, :].partition_broadcast(128)), w=["esink"], dma=True)
        tr.op(ACT, lambda: nc.scalar.activation(out=esink[:], in_=esink[:], func=AF.Exp), r=["esink"], w=["esink"])

    def mem_kv():
        for pair in range(2):
            b = bank()
            mm_group(ps[b][:, 0:256], [(wmk_sb[:, k, pair * 128:(pair + 1) * 128], memT[:, k, :]) for k in range(8)],
                     bk(b), ["memT", ("z", 0), ("z", 1)] + [("wmk", k) for k in range(8)])
            tr.op(ACT, lambda b=b, pair=pair: nc.scalar.copy(out=KmT[:, pair, :], in_=ps[b][:, 0:256]), r=[bk(b)], w=["KmT"])
        for mb in range(2):
            b = bank()
            mm_group(ps[b][:, 0:256], [(memT[:, k, mb * 128:(mb + 1) * 128], wmk_sb[:, k, 256:512]) for k in range(8)],
                     bk(b), ["memT", ("z", 0), ("z", 1)] + [("wmk", k) for k in range(8)])
            tr.op(ACT, lambda b=b, mb=mb: nc.scalar.copy(out=VmA[:, mb, :, 0:64], in_=ps[b][:, 0:256].rearrange("p (h d) -> p h d", h=4)),
                  r=[bk(b)], w=["VmA"])

    def issue_loads(l, t, src, skey, slot, par):
        tr.op(POOL, lambda: nc.gpsimd.dma_start(out=xb[slot][:], in_=src[t * 128:(t + 1) * 128, :]),
              r=[(skey, t)], w=[("xb", slot)], dma=True)
        tr.op(SP, lambda: nc.sync.dma_start(out=z[par][:], in_=src[t * 128:(t + 1) * 128, :]),
              r=[(skey, t)], w=[("z", par)], dma=True)

    def xT_step(slot):
        b = bankA()
        transposes([(psb[b][:, k * 128:(k + 1) * 128], xb[slot][:, k * 128:(k + 1) * 128]) for k in range(8)],
                   bk(b), [("xb", slot)])
        tr.op(ACT, lambda b=b: nc.scalar.copy(out=xT[:].rearrange("p k c -> p (k c)"), in_=psb[b][:, 0:1024]), r=[bk(b)], w=["xT"])

    def stageA(l, t, slot, par):
        sA = t % 8
        sC = t % 4
        aqT, mqT, cqT, QT, sg = aqT_l[par], mqT_l[par], cqT_l[par], QT_l[par], sg_l[par]

        def tm(c0, n):
            b_ = bankA()
            mm_group(ps[b_][:, 0:n], [(xT[:, k, :], w_in_sb[:, k, c0:c0 + n]) for k in range(8)], bk(b_), ["xT"] + WINK)
            return b_

        def fm(c0, nchunks):
            b_ = bankA()
            for j in range(nchunks):
                mm_group(ps[b_][:, j * 128:(j + 1) * 128],
                         [(w_in_sb[:, k, c0 + j * 128:c0 + (j + 1) * 128], xT[:, k, :]) for k in range(8)],
                         bk(b_), ["xT"] + WINK)
            return b_

        b3 = tm(T3, 480)
        b4 = tm(T4, 128)
        tr.op(DVE, lambda: nc.vector.bn_stats(st6[:, 0, :], ps[b3][:, 256:448]), r=[bk(b3)], w=["st6a"])
        tr.op(DVE, lambda: nc.vector.bn_aggr(mv[:, 0, :], st6[:, 0, :]), r=["st6a"], w=["mv0"])
        tr.op(DVE, lambda: nc.vector.scalar_tensor_tensor(out=sm[:, 0:1], in0=mv[:, 0, 0:1], scalar=mv[:, 0, 0:1], in1=mv[:, 0, 1:2],
                                                         op0=ALU.mult, op1=ALU.add), r=["mv0"], w=["sm0"])
        rsqrt_chain("sm0", sm[:, 0:1], 1e-6, sm[:, 1:2], "sm1")
        tr.op(DVE, lambda: nc.vector.bn_stats(st6[:, 1, :], ps[b4][:, 0:128]), r=[bk(b4)], w=["st6b"])
        tr.op(DVE, lambda: nc.vector.bn_aggr(mv[:, 1, :], st6[:, 1, :]), r=["st6b"], w=["mv1"])
        tr.op(DVE, lambda: nc.vector.scalar_tensor_tensor(out=sm[:, 2:3], in0=mv[:, 1, 0:1], scalar=mv[:, 1, 0:1], in1=mv[:, 1, 1:2],
                                                         op0=ALU.mult, op1=ALU.add), r=["mv1"], w=["sm2"])
        rsqrt_chain("sm2", sm[:, 2:3], 1e-6, sm[:, 3:4], "sm3")
        tr.op(ACT, lambda: nc.scalar.copy(out=cqn[:], in_=ps[b3][:, 256:448]), r=[bk(b3)], w=["cqn"])
        tr.op(ACT, lambda: nc.scalar.copy(out=ckvn[:], in_=ps[b4][:, 0:128]), r=[bk(b4)], w=["ckvn"])
        tr.op(ACT, lambda: nc.scalar.copy(out=V_A[:, sA, :, 0:64], in_=ps[b3][:, 0:256].rearrange("p (h d) -> p h d", h=4)),
              r=[bk(b3)], w=[("V_A", sA)])
        rope(ps[b3][:, 448:480].rearrange("p (o two f) -> p o two f", o=1, two=2), 1, 16, cosB[:, t, :], sinB[:, t, :],
             kr[:, 0:32].rearrange("p (o two f) -> p o two f", o=1, two=2), [bk(b3), "cosB", "sinB"], ["kr"])
        tr.op(DVE, lambda: nc.vector.tensor_copy(K_tok[:, :, 64:96], kr[:].unsqueeze(1).broadcast_to([128, 4, 32])),
              r=["kr"], w=["K_tok"])
        b2 = tm(T2, 512)
        v6 = ps[b2][:, 0:384].rearrange("p (h two f) -> p h two f", h=6, two=2)
        rope(v6, 6, 32, cosC[:, t, :], sinC[:, t, :], cqk_tok[:, :, :, :], [bk(b2), "cosC", "sinC"], ["cqk_tok"])
        tr.op(ACT, lambda: nc.scalar.copy(out=V_C[:, sC, :, 0:64], in_=ps[b2][:, 384:512].rearrange("p (h d) -> p h d", h=2)),
              r=[bk(b2)], w=[("V_C", sC)])
        bA = fm(FM0, 4)
        tr.op(DVE, lambda: nc.vector.tensor_copy(aqT[:].rearrange("p (a two) c -> p a two c", two=2)[0:64, :, 0, :],
                                                 ps[bA][0:64, 0:256].rearrange("p (a c) -> p a c", a=2)), r=[bk(bA)], w=[("aqT", par)])
        tr.op(DVE, lambda: nc.vector.tensor_copy(aqT[:].rearrange("p (a two) c -> p a two c", two=2)[64:128, :, 1, :],
                                                 ps[bA][64:128, 0:256].rearrange("p (a c) -> p a c", a=2)), r=[bk(bA)], w=[("aqT", par)])
        tr.op(DVE, lambda: nc.vector.tensor_copy(KT_A[:, :, sA, :], ps[bA][:, 256:512].rearrange("p (a c) -> p a c", a=2)),
              r=[bk(bA)], w=[("KT_A", sA)])
        bM = fm(FM0 + 512, 2)
        tr.op(DVE, lambda: nc.vector.tensor_copy(mqT[:].rearrange("p (a two) c -> p a two c", two=2)[0:64, :, 0, :],
                                                 ps[bM][0:64, 0:256].rearrange("p (a c) -> p a c", a=2)), r=[bk(bM)], w=[("mqT", par)])
        tr.op(DVE, lambda: nc.vector.tensor_copy(mqT[:].rearrange("p (a two) c -> p a two c", two=2)[64:128, :, 1, :],
                                                 ps[bM][64:128, 0:256].rearrange("p (a c) -> p a c", a=2)), r=[bk(bM)], w=[("mqT", par)])
        for gi, c0 in enumerate((G0, G1)):
            bg = tm(c0, 512)
            tr.op(ACT, lambda bg=bg, gi=gi: nc.scalar.activation(out=sg[:, gi * 512:(gi + 1) * 512], in_=ps[bg][:, :], func=AF.Tanh, scale=0.5),
                  r=[bk(bg)], w=[("sg", par, gi)])
            tr.op(DVE, lambda bg=bg, gi=gi: nc.vector.scalar_tensor_tensor(out=sg[:, gi * 512:(gi + 1) * 512], in0=sg[:, gi * 512:(gi + 1) * 512],
                                                                          scalar=1.0, in1=ps[bg][:, :], op0=ALU.add, op1=ALU.mult),
                  r=[bk(bg), ("sg", par, gi)], w=[("sg", par, gi)])
        bt = bankA()
        transposes([(psb[bt][:, 0:128], cqn[:, 0:128]), (psb[bt][0:64, 128:256], cqn[:, 128:192]), (psb[bt][:, 256:384], ckvn[:, :])],
                   bk(bt), ["cqn", "ckvn"])
        tr.op(DVE, lambda: nc.vector.tensor_scalar(out=cqnT[:, 0, :], in0=psb[bt][:, 0:128], scalar1=gq_sb[:, 0:1], scalar2=None, op0=ALU.mult),
              r=[bk(bt), "gq"], w=["cqnT"])
        tr.op(DVE, lambda: nc.vector.tensor_scalar(out=cqnT[0:64, 1, :], in0=psb[bt][0:64, 128:256], scalar1=gq_sb[0:64, 1:2], scalar2=None, op0=ALU.mult),
              r=[bk(bt), "gq"], w=["cqnT"])
        tr.op(DVE, lambda: nc.vector.tensor_scalar(out=ckvnT[:, :], in0=psb[bt][:, 256:384], scalar1=gkv_sb[:, 0:1], scalar2=None, op0=ALU.mult),
              r=[bk(bt), "gkv"], w=["ckvnT"])
        bq = bankA()
        mm_group(ps[bq][:, 0:384], [(cqnT[:, 0, :], w_uq_sb[:, 0, :]), (cqnT[:, 1, :], w_uq_sb[:, 1, :])], bk(bq), ["cqnT", "w_uq"])
        bkv = bankA()
        mm_group(ps[bkv][:, 0:512], [(ckvnT[:, :], w_ukv_sb[:, :])], bk(bkv), ["ckvnT", "w_ukv"])
        q4 = ps[bq][:, 0:384].rearrange("p (h c) -> p h c", h=4)
        tr.op(ACT, lambda: nc.scalar.activation(out=Q_tok[:, :, 0:64], in_=q4[:, :, 0:64], func=AF.Copy, scale=sm[:, 1:2]), r=[bk(bq), "sm1"], w=["Q_tok"])
        rope(q4[:, :, 64:96].rearrange("p h (two f) -> p h two f", two=2), 4, 16, cosB[:, t, :], sinB[:, t, :],
             Q_tok[:, :, 64:96].rearrange("p h (two f) -> p h two f", two=2), [bk(bq), "cosB", "sinB"], [("Q_tok")],
             scale_ap=sm[:, 1:2], scale_key="sm1")
        kv4 = ps[bkv][:, 0:512].rearrange("p (h c) -> p h c", h=4)
        tr.op(ACT, lambda: nc.scalar.activation(out=K_tok[:, :, 0:64], in_=kv4[:, :, 0:64], func=AF.Copy, scale=sm[:, 3:4]), r=[bk(bkv), "sm3"], w=["K_tok"])
        tr.op(ACT, lambda: nc.scalar.activation(out=V_B[:, t, :, 0:64], in_=kv4[:, :, 64:128], func=AF.Copy, scale=sm[:, 3:4]), r=[bk(bkv), "sm3"], w=[("V_B", t)])
        bc = bankA()
        ck2 = cqk_tok[:].rearrange("p h two f -> p (h two f)")
        transposes([(psb[bc][:, j * 128:(j + 1) * 128], ck2[:, j * 128:(j + 1) * 128]) for j in range(3)], bk(bc), ["cqk_tok"])
        tr.op(DVE, lambda: nc.vector.tensor_copy(cqT[0:64, 0:2, :], psb[bc][0:64, 0:256].rearrange("p (a c) -> p a c", a=2)), r=[bk(bc)], w=[("cqT", par)])
        tr.op(DVE, lambda: nc.vector.tensor_copy(cqT[64:128, 2:4, :], psb[bc][64:128, 0:256].rearrange("p (a c) -> p a c", a=2)), r=[bk(bc)], w=[("cqT", par)])
        tr.op(DVE, lambda: nc.vector.tensor_copy(KT_C[:, sC, :], psb[bc][:, 256:384]), r=[bk(bc)], w=[("KT_C", sC)])
        bqt = bankA()
        transposes([(psb[bqt][0:96, h * 128:(h + 1) * 128], Q_tok[:, h, :]) for h in range(4)], bk(bqt), ["Q_tok"])
        tr.op(ACT, lambda: nc.scalar.copy(out=QT[0:96, :, :].rearrange("p h c -> p (h c)"), in_=psb[bqt][0:96, 0:512]), r=[bk(bqt)], w=[("QT", par)])
        bkt = bankA()
        transposes([(psb[bkt][0:96, h * 128:(h + 1) * 128], K_tok[:, h, :]) for h in range(4)], bk(bkt), ["K_tok"])
        tr.op(ACT, lambda: nc.scalar.copy(out=KT_B[0:96, :, t * 128:(t + 1) * 128],
                                          in_=psb[bkt][0:96, 0:512].rearrange("p (h c) -> p h c", h=4)),
              r=[bk(bkt)], w=[("KT_B", t)])


    def stageB(l, t, dst, dkey, par):
        sA = t % 8
        sC = t % 4
        aqT, mqT, cqT, QT, sg = aqT_l[par], mqT_l[par], cqT_l[par], QT_l[par], sg_l[par]
        def oslice(g, h):
            n = g * 4 + h
            return n // 7, (n % 7) * 65

        def zinit():
            ins = None
            for i in range(3):
                ins = nc.tensor.matmul(ps[i][:, :], zeros[:, 0:128], w_out_sb[:, 0, 0:512], start=True, stop=True)
            return ins
        tr.op(PE, zinit, r=["zeros", ("w_out", 0)], w=[bk(0), bk(1), bk(2)])

        blocks = []
        for mb in range(2):
            qk = [(KmT[:, h // 2, mb * 128:(mb + 1) * 128], mqT[:, h, :]) for h in range(4)]
            blocks.append((qk, ["KmT", ("mqT", par)], "full", 0.125, None, [VmA[:, mb, h, :] for h in range(4)], ["VmA"], 3))
        for jb in range(max(0, t - 4), t + 1):
            j = jb - (t - 4)
            sl = jb % 8
            qk = [(KT_A[:, h // 2, sl, :], aqT[:, h, :]) for h in range(4)]
            mode = "lo" if j == 0 else ("hi" if j == 4 else "full")
            ti = 0 if j <= 2 else (1 if j == 3 else 2)
            blocks.append((qk, [("KT_A", sl), ("aqT", par)], mode, 0.125, ti, [V_A[:, sl, h, :] for h in range(4)], [("V_A", sl)], 0))
        for jb in range(max(0, t - 1), t + 1):
            sl = jb % 4
            qk = [(KT_C[:, sl, :], cqT[:, h, :]) for h in range(4)]
            mode = "hi" if jb == t else "lo"
            blocks.append((qk, [("KT_C", sl), ("cqT", par)], mode, 0.125, None, [V_C[:, sl, h // 2, :] for h in range(4)], [("V_C", sl)], 2))
        for jb in range(0, t + 1):
            qk = [(KT_B[0:96, h, jb * 128:(jb + 1) * 128], QT[0:96, h, :]) for h in range(4)]
            mode = "hi" if jb == t else "full"
            blocks.append((qk, [("KT_B", jb), ("QT", par)], mode, SCALE_B, None, [V_B[:, jb, h, :] for h in range(4)], [("V_B", jb)], 1))

        nblk = len(blocks)
        pending = []

        def emit_pv(item):
            pt, ptk, vl, vk, g, last = item

            def fn():
                ins = None
                for h in range(4):
                    bi, off = oslice(g, h)
                    ins = nc.tensor.matmul(ps[bi][:, off:off + 65], pt[:, h * 128:(h + 1) * 128], vl[h], start=False, stop=last,
                                           skip_group_check=True)
                return ins
            okeys = sorted({bk(oslice(g, h)[0]) for h in range(4)})
            tr.op(PE, fn, r=[ptk] + vk, w=okeys)

        for bi_, (qk, qkeys, mode, scale, ti, vl, vk, g) in enumerate(blocks):
            b = bankB()

            def fqk(b=b, qk=qk, ti=ti):
                ins = None
                if ti is None:
                    for h in range(4):
                        ins = nc.tensor.matmul(ps[b][:, h * 128:(h + 1) * 128], qk[h][0], qk[h][1], start=True, stop=True)
                    return ins
                for h in range(4):
                    nc.tensor.matmul(ps[b][:, h * 128:(h + 1) * 128], qk[h][0], qk[h][1], start=(h == 0), stop=False,
                                     skip_group_check=True)
                nc.tensor.matmul(ps[b][:, :], ident[:, :], tblh[:, ti, :], start=False, stop=False, skip_group_check=True)
                return nc.tensor.matmul(ps[b][:, :], ident[:, :], tbll[:, ti, :], start=False, stop=True, skip_group_check=True)
            tr.op(PE, fqk, r=qkeys + ([] if ti is None else [("tblh", ti), ("tbll", ti), "ident"]), w=[bk(b)])
            src_ap, src_key, sc = ps[b], bk(b), scale
            if mode == "full":
                i_ = pt_rr[0] % NPT
                pt_rr[0] += 1
                pt, ptk = PT[i_], ("PT", i_)
                tr.op(ACT, lambda pt=pt, src_ap=src_ap, sc=sc: nc.scalar.activation(out=pt[:, :], in_=src_ap[:, :], func=AF.Exp, scale=sc),
                      r=[src_key], w=[ptk])
            else:
                if mode == "lo":
                    i_ = lo_rr[0] % 1
                    lo_rr[0] += 1
                    pt, ptk = PT_lo[i_], ("PTlo", i_)
                    full_p, part_p, q0 = slice(64, 128), slice(0, 64), 0
                else:
                    i_ = hi_rr[0] % 3
                    hi_rr[0] += 1
                    pt, ptk = PT_hi[i_], ("PThi", i_)
                    full_p, part_p, q0 = slice(0, 64), slice(64, 128), 64

                def fexp(pt=pt, src_ap=src_ap, sc=sc, full_p=full_p, part_p=part_p, q0=q0):
                    nc.scalar.activation(out=pt[full_p, :], in_=src_ap[full_p, :], func=AF.Exp, scale=sc)
                    return nc.scalar.activation(out=pt[part_p, :].rearrange("p (h q) -> p h q", h=4)[:, :, q0:q0 + 64],
                                                in_=src_ap[part_p, :].rearrange("p (h q) -> p h q", h=4)[:, :, q0:q0 + 64],
                                                func=AF.Exp, scale=sc)
                tr.op(ACT, fexp, r=[src_key], w=[ptk])
            last = (bi_ == nblk - 1) or (blocks[bi_ + 1][7] != g)
            pending.append((pt, ptk, vl, vk, g, last))
            if len(pending) > 2:
                emit_pv(pending.pop(0))
        while pending:
            emit_pv(pending.pop(0))

        for i in range(3):
            nsl = 7 if i < 2 else 2
            tr.op(DVE, lambda i=i, nsl=nsl: nc.vector.tensor_copy(den[:, i * 7:i * 7 + nsl],
                                                                  ps[i][:, 0:nsl * 65].rearrange("p (n c) -> p n c", c=65)[:, :, 64]),
                  r=[bk(i)], w=["den"])
        tr.op(DVE, lambda: nc.vector.tensor_tensor(den[:, 8:12], den[:, 8:12], esink[:, :], ALU.add), r=["den", "esink"], w=["den"])
        tr.op(DVE, lambda: nc.vector.tensor_scalar(out=den[:, :], in0=den[:, :], scalar1=2.0, scalar2=None, op0=ALU.mult), r=["den"], w=["den"])
        tr.op(DVE, lambda: nc.vector.reciprocal(rec[:, :], den[:, :]), r=["den"], w=["rec"])

        def fgate():
            ins = None
            for n in range(16):
                bi, off = n // 7, (n % 7) * 65
                ins = nc.vector.scalar_tensor_tensor(out=att[:, n * 64:(n + 1) * 64], in0=ps[bi][:, off:off + 64], scalar=rec[:, n:n + 1],
                                                     in1=sg[:, n * 64:(n + 1) * 64], op0=ALU.mult, op1=ALU.mult)
            return ins
        tr.op(DVE, fgate, r=[bk(0), bk(1), bk(2), "rec", ("sg", par, 0), ("sg", par, 1)], w=["att"])
    def stageB2(l, t, dst, dkey, par):
        bg_ = bankC()
        transposes([(psb[bg_][:, k * 128:(k + 1) * 128], att[:, k * 128:(k + 1) * 128]) for k in range(8)], bk(bg_), ["att"])
        tr.op(ACT, lambda: nc.scalar.copy(out=gT[:].rearrange("p k c -> p (k c)"), in_=psb[bg_][:, 0:1024]), r=[bk(bg_)], w=["gT"])
        zt = z[par]
        for n in range(2):
            by = bankC()
            mm_group(ps[by][:, :], [(gT[:, k, :], w_out_sb[:, k, n * 512:(n + 1) * 512]) for k in range(8)], bk(by), ["gT"] + [("w_out", k) for k in range(8)])
            tr.op(DVE, lambda by=by, n=n: nc.vector.scalar_tensor_tensor(out=zt[:, n * 512:(n + 1) * 512], in0=zt[:, n * 512:(n + 1) * 512],
                                                                        scalar=ALPHA, in1=ps[by][:, :], op0=ALU.mult, op1=ALU.add),
                  r=[bk(by), ("z", par)], w=[("z", par)])
        def fstats():
            nc.vector.bn_stats(st6[:, 2, :], zt[:, 0:512])
            return nc.vector.bn_stats(st6[:, 3, :], zt[:, 512:1024])
        tr.op(DVE, fstats, r=[("z", par)], w=["st6c"])
        tr.op(DVE, lambda: nc.vector.bn_aggr(mv[:, 2, :], st6[:, 2:4, :].rearrange("p a b -> p (a b)")), r=["st6c"], w=["mv2"])
        rsqrt_chain("mv2", mv[:, 2, 1:2], 1e-5, sm[:, 4:5], "sm4")
        tr.op(DVE, lambda: nc.vector.scalar_tensor_tensor(out=sm[:, 5:6], in0=mv[:, 2, 0:1], scalar=-1.0, in1=sm[:, 4:5],
                                                         op0=ALU.mult, op1=ALU.mult), r=["mv2", "sm4"], w=["sm5"])
        tr.op(POOL, lambda: nc.gpsimd.tensor_scalar(out=zt[:, :], in0=zt[:, :], scalar1=sm[:, 4:5], scalar2=sm[:, 5:6],
                                                    op0=ALU.mult, op1=ALU.add),
              r=[("z", par), "sm4", "sm5"], w=[("z", par)])
        tr.op(POOL, lambda: nc.gpsimd.tensor_tensor(zt[:, :], zt[:, :], G_sb[:, :], ALU.mult), r=[("z", par), "G"], w=[("z", par)])
        tr.op(POOL, lambda: nc.gpsimd.tensor_tensor(zt[:, :], zt[:, :], B_sb[:, :], ALU.add), r=[("z", par), "Bv"], w=[("z", par)])
        tr.op(SP, lambda: nc.sync.dma_start(out=dst[t * 128:(t + 1) * 128, :], in_=zt[:, :]),
              r=[("z", par)], w=[(dkey, t)], dma=True)

    AFRAC = 0.95
    BOFF = 0.15
    import os as _os
    MERGE_OFF = bool(_os.environ.get('KMERGE_OFF'))

    def merge(LA, LB):
        if MERGE_OFF:
            return list(LA) + list(LB)
        na, nb = max(len(LA), 1), max(len(LB), 1)
        items = [(AFRAC * (i + 0.5) / na, 0, i, a) for i, a in enumerate(LA)]
        items += [((i + 0.5) / nb, 1, i, b) for i, b in enumerate(LB)]
        items.sort(key=lambda x_: (x_[0], x_[1], x_[2]))
        return [x_[3] for x_ in items]

    def load_xb(src, skey, t, slot):
        tr.op(POOL, lambda: nc.gpsimd.dma_start(out=xb[slot][:], in_=src[t * 128:(t + 1) * 128, :]),
              r=[(skey, t)], w=[("xb", slot)], dma=True)

    def load_z(src, skey, t, par):
        tr.op(SP, lambda: nc.sync.dma_start(out=z[par][:], in_=src[t * 128:(t + 1) * 128, :]),
              r=[(skey, t)], w=[("z", par)], dma=True)

    def merge3(LA, LB, LC):
        if MERGE_OFF:
            return list(LC) + list(LA) + list(LB)
        items = []
        for off, frac, pri, L in ((0.0, AFRAC, 1, LA), (BOFF, 1.0 - BOFF, 2, LB), (0.20, 0.45, 0, LC)):
            n = max(len(L), 1)
            items += [(off + frac * (i + 0.5) / n, pri, i, a) for i, a in enumerate(L)]
        items.sort(key=lambda x_: (x_[0], x_[1], x_[2]))
        return [x_[3] for x_ in items]

    sl_list = [(s_, l_) for s_ in range(nseq) for l_ in range(nl)]

    def io(s_, l_):
        src_ = x_d[s_] if l_ == 0 else hb_d[(l_ - 1) % 2]
        dst_ = out_d[s_] if l_ == nl - 1 else hb_d[l_ % 2]
        skey_ = ("xin", s_) if l_ == 0 else ("hb", (l_ - 1) % 2)
        dkey_ = ("xout", s_) if l_ == nl - 1 else ("hb", l_ % 2)
        return src_, dst_, skey_, dkey_

    early_w = set()
    early_x = set()

    def early(idx, gt0):
        s_, l_ = sl_list[idx]
        src_, _, skey_, _ = io(s_, l_)
        if l_ != 0 and ntiles > 0:
            load_xb(src_, skey_, 0, gt0 % 2)
            if ntiles > 1:
                load_xb(src_, skey_, 1, (gt0 + 1) % 2)
            early_x.add(idx)
        layer_setup_early(l_)
        early_w.add(idx)

    gt = 0
    for idx, (s, l) in enumerate(sl_list):
        if l == 0:
            seq_setup(s)
        src, dst, skey, dkey = io(s, l)
        if idx not in early_w:
            layer_setup_early(l)
        layer_setup_late(l)
        mem_kv()
        if idx not in early_x and ntiles > 0:
            load_xb(src, skey, 0, gt % 2)
            if ntiles > 1:
                load_xb(src, skey, 1, (gt + 1) % 2)
        if ntiles > 0:
            load_z(src, skey, 0, gt % 2)
            xT_step(gt % 2)
            stageA(l, 0, gt % 2, gt % 2)
            if ntiles > 1:
                xT_step((gt + 1) % 2)
        for t in range(ntiles):
            LA, LC = [], []
            if t + 1 < ntiles:
                tr.begin()
                if t + 2 < ntiles:
                    load_xb(src, skey, t + 2, gt % 2)
                stageA(l, t + 1, (gt + 1) % 2, (gt + 1) % 2)
                if t + 2 < ntiles:
                    xT_step(gt % 2)
                LA = tr.end()
            if t > 0:
                tr.begin()
                stageB2(l, t - 1, dst, dkey, (gt - 1) % 2)
                LC = tr.end()
            tr.begin()
            stageB(l, t, dst, dkey, gt % 2)
            LB = tr.end()
            tr.commit(merge3(LA, LB, LC))
            if t + 1 < ntiles:
                load_z(src, skey, t + 1, (gt + 1) % 2)
            gt += 1
            if t == ntiles - 2 and idx + 1 < len(sl_list):
                early(idx + 1, gt + 1)
        if ntiles > 0:
            stageB2(l, ntiles - 1, dst, dkey, (gt - 1) % 2)
    tr.emit()
    return nc, es


def _prep_shared(w_in, rel_bias, mla_q_norm, w_uq, mla_kv_norm, w_ukv, swa_sinks, w_mem_kv, w_out, ln_gain, ln_bias):
    r = lambda a, b: np.arange(a, b)
    cq_perm = np.concatenate([1632 + np.arange(64) + 64 * h for h in (0, 2, 1, 3)])
    cols = np.concatenate([
        r(0, 256), r(256, 512), r(2400, 2656),
        r(768, 1024), r(1376, 1632),
        r(2144, 2400), r(2656, 2912),
        cq_perm, r(1888, 2016), r(2016, 2144),
        r(512, 768), r(1024, 1216), r(1344, 1376),
        r(1216, 1344),
    ])
    assert cols.shape[0] == PW and np.unique(cols).shape[0] == PW
    w_in_p = np.ascontiguousarray(np.take(np.asarray(w_in, np.float32), cols, axis=2))
    rb = np.asarray(rel_bias, np.float32)
    i = np.arange(128)[:, None]
    q = np.arange(128)[None, :]
    idx3 = np.minimum(256 + q - i, 256)
    idx4 = 128 + q - i
    idx0 = np.full((128, 128), 256)
    idx = np.stack([idx0, idx3, idx4], axis=1)
    tbl = rb[:, :, idx]
    tbl = np.ascontiguousarray(np.transpose(tbl, (0, 2, 3, 1, 4))).reshape(NL, 128, 3, 512)
    ident = np.eye(128, dtype=np.float32)
    invfC = (10000.0 ** (-np.arange(0, 64, 2, dtype=np.float32) / 64)).astype(np.float32)
    invfB = (10000.0 ** (-np.arange(0, 32, 2, dtype=np.float32) / 32)).astype(np.float32)
    invf = np.ascontiguousarray(np.broadcast_to(np.concatenate([invfC, invfB])[None, :], (128, 48))).astype(np.float32)
    f = lambda a: np.ascontiguousarray(np.asarray(a, np.float32))
    return {"w_in": w_in_p, "relb": tbl, "qn": f(mla_q_norm), "w_uq": f(w_uq), "kvn": f(mla_kv_norm), "w_ukv": f(w_ukv),
            "sinks": f(swa_sinks), "w_mem_kv": f(w_mem_kv), "w_out": f(w_out), "ln_gain": f(ln_gain), "ln_bias": f(ln_bias),
            "ident": ident, "invf": invf}


_CACHE = {}


def kernel(x, mem, positions, w_in, rel_bias, mla_q_norm, w_uq, mla_kv_norm, w_ukv, swa_sinks, w_mem_kv, w_out, ln_gain, ln_bias):
    shared = _prep_shared(w_in, rel_bias, mla_q_norm, w_uq, mla_kv_norm, w_ukv, swa_sinks, w_mem_kv, w_out, ln_gain, ln_bias)
    x = np.asarray(x, np.float32)
    mem = np.asarray(mem, np.float32)
    positions = np.asarray(positions, np.int32)
    in_maps = []
    for c in range(NCORES):
        sl = slice(c * NSEQ, (c + 1) * NSEQ)
        pos = np.ascontiguousarray(np.transpose(positions[sl].reshape(NSEQ, T, 128), (0, 2, 1)))
        m = {"x": np.ascontiguousarray(x[sl]), "mem": np.ascontiguousarray(mem[sl]), "pos": pos}
        m.update(shared)
        in_maps.append(m)
    if "nc" not in _CACHE:
        _CACHE["nc"] = build_program()
    nc, _es = _CACHE["nc"]
    res = run_bass_kernel_spmd(nc, in_maps, core_ids=list(range(NCORES)))
    out = np.concatenate([np.asarray(r["out"], np.float32) for r in res.results], axis=0)
    return out
```
